# Optimizing a Trainium2 kernel written in Bass

```python
import math, functools
import jax, jax.numpy as jnp
from jax import lax
import numpy as np

D_MODEL = 2048
BATCH = 16
SEQ = 2048
DEPTH = 1
DEC_BATCH = 32
DEC_SEQ = 4
PAST_LEN = 16384
PAGE_SIZE = 128

MIX_WIDTH = D_MODEL
ATT_WIDTH = MIX_WIDTH // 2
LRU_WIDTH = MIX_WIDTH - ATT_WIDTH
HEAD_DIM = 128
N_ATT_HEADS = ATT_WIDTH // HEAD_DIM
LRU_BLOCKS = 8
LRU_BLOCK_W = LRU_WIDTH // LRU_BLOCKS
CONV_WIDTH = 4
LRU_C = 8.0
DILATED_PATTERNS = ((128, 1), (512, 4), (2048, 16))
MAX_WINDOW = max(w for w, _ in DILATED_PATTERNS)
N_BUCKETS = 32
MAX_DISTANCE = MAX_WINDOW
D_FF = ((8 * D_MODEL // 3 + 255) // 256) * 256
IN_WIDTH = 3 * ATT_WIDTH + 2 * LRU_WIDTH
RMS_EPS = 1e-6
NEG_INF = -1e30
ATT_SCALE = 1.0 / math.sqrt(HEAD_DIM)

kernel_name = "hybrid_dilated_attn_rglru_macaron_step"


def _rms_norm(x, g):
    xf = x.astype(jnp.float32)
    y = xf * lax.rsqrt(jnp.mean(xf * xf, axis=-1, keepdims=True) + RMS_EPS)
    return (y * g.astype(jnp.float32)).astype(x.dtype)


def _swiglu(h, w_gate, w_up, w_down):
    return (jax.nn.silu(h @ w_gate) * (h @ w_up)) @ w_down


def _rel_bucket(dist):
    max_exact = N_BUCKETS // 2
    df = jnp.maximum(dist, 1).astype(jnp.float32)
    large = max_exact + (jnp.log(df / max_exact) / math.log(MAX_DISTANCE / max_exact)
                         * (N_BUCKETS - max_exact)).astype(jnp.int32)
    return jnp.where(dist < max_exact, dist, jnp.minimum(large, N_BUCKETS - 1))


def _pattern_bias(rel_bias, window, dil):
    dist = dil * jnp.arange(window // dil + 1, dtype=jnp.int32)
    return rel_bias[_rel_bucket(dist)].T.astype(jnp.float32)


def _dilated_prompt(q, k, v, bias_j, window, dil):
    B, S, H, E = q.shape
    n = window // dil
    nb = -(-S // (dil * n))
    Sp = nb * n * dil

    def split(t):
        t = jnp.pad(t, ((0, 0), (0, Sp - S), (0, 0), (0, 0)))
        return t.reshape(B, nb, n, dil, H, E)

    def with_prev(t):
        prev = jnp.concatenate([jnp.zeros_like(t[:, :1]), t[:, :-1]], axis=1)
        return jnp.concatenate([prev, t], axis=2)

    qb = split(q)
    kc = with_prev(split(k))
    vc = with_prev(split(v))
    qq = jnp.arange(n)[:, None]
    kk = jnp.arange(2 * n)[None, :]
    j = n + qq - kk
    band = (j >= 0) & (j <= n)
    not_before_start = (jnp.arange(nb) > 0)[:, None, None] | (kk >= n)[None]
    valid = band[None] & not_before_start
    bias = bias_j[:, jnp.clip(j, 0, n)]
    s = jnp.einsum('bcqrhe,bckrhe->bcrhqk', qb, kc).astype(jnp.float32) * ATT_SCALE
    s = s + bias[None, None, None]
    s = jnp.where(valid[None, :, None, None], s, NEG_INF)
    lse = jax.nn.logsumexp(s, axis=-1)
    p = jnp.exp(s - lse[..., None]).astype(v.dtype)
    o = jnp.einsum('bcrhqk,bckrhe->bcqrhe', p, vc).reshape(B, Sp, H, E)[:, :S]
    lse = lse.transpose(0, 1, 4, 2, 3).reshape(B, Sp, H)[:, :S]
    return o, lse


def _dilated_sample(q, k_all, v_all, bias_j, window, dil):
    B, T = q.shape[:2]
    buf = k_all.shape[1] - T
    n = window // dil
    idx = buf + jnp.arange(T)[:, None] - dil * jnp.arange(n + 1)[None, :]
    valid = idx >= 0
    safe = jnp.maximum(idx, 0)
    kg = k_all[:, safe]
    vg = v_all[:, safe]
    s = jnp.einsum('bthe,btjhe->bhtj', q, kg).astype(jnp.float32) * ATT_SCALE
    s = s + bias_j[:, None, :]
    s = jnp.where(valid[None, None], s, NEG_INF)
    lse = jax.nn.logsumexp(s, axis=-1)
    p = jnp.exp(s - lse[..., None]).astype(v_all.dtype)
    o = jnp.einsum('bhtj,btjhe->bthe', p, vg)
    return o, lse.transpose(0, 2, 1)


def _mix_patterns(outs, lses):
    alpha = jax.nn.softmax(jnp.stack(lses), axis=0)
    o = jnp.einsum('pbth,pbthe->bthe', alpha, jnp.stack(outs).astype(jnp.float32))
    return o.astype(outs[0].dtype)


def _attend_prompt(q, k, v, rel_bias):
    outs, lses = [], []
    for window, dil in DILATED_PATTERNS:
        o, l = _dilated_prompt(q, k, v, _pattern_bias(rel_bias, window, dil), window, dil)
        outs.append(o)
        lses.append(l)
    return _mix_patterns(outs, lses)


def _attend_sample(q, k, v, k_past, v_past, rel_bias):
    k_all = jnp.concatenate([k_past.astype(k.dtype), k], axis=1)
    v_all = jnp.concatenate([v_past.astype(v.dtype), v], axis=1)
    outs, lses = [], []
    for window, dil in DILATED_PATTERNS:
        o, l = _dilated_sample(q, k_all, v_all, _pattern_bias(rel_bias, window, dil), window, dil)
        outs.append(o)
        lses.append(l)
    return _mix_patterns(outs, lses)


def _rg_lru_branch(xl, gate, conv_prefix, h0, conv_w, conv_b, w_a, b_a, w_x, b_x, lam):
    B, T, R = xl.shape
    xc = jnp.concatenate([conv_prefix.astype(xl.dtype), xl], axis=1)
    u = conv_b + sum(conv_w[j] * xc[:, j:j + T] for j in range(CONV_WIDTH))
    ub = u.reshape(B, T, LRU_BLOCKS, LRU_BLOCK_W)
    r = jax.nn.sigmoid(jnp.einsum('btnc,ncd->btnd', ub, w_a).reshape(B, T, R) + b_a)
    i = jax.nn.sigmoid(jnp.einsum('btnc,ncd->btnd', ub, w_x).reshape(B, T, R) + b_x)
    log_a = -LRU_C * r.astype(jnp.float32) * jax.nn.softplus(-lam.astype(jnp.float32))
    a = jnp.exp(log_a)
    bx = jnp.sqrt(-jnp.expm1(2.0 * log_a)) * (i * u).astype(jnp.float32)

    def step(h, ab):
        h = ab[0] * h + ab[1]
        return h, h

    h_last, hs = lax.scan(step, h0.astype(jnp.float32), (a.swapaxes(0, 1), bx.swapaxes(0, 1)))
    y = hs.swapaxes(0, 1).astype(xl.dtype) * jax.nn.gelu(gate)
    return y, xc[:, -(CONV_WIDTH - 1):], h_last


def _layer(x, p, attend, conv_prefix, h0):
    x = x + 0.5 * _swiglu(_rms_norm(x, p['g_ffn1']), p['w1_gate'], p['w1_up'], p['w1_down'])
    h = _rms_norm(x, p['g_mix'])
    B, T, _ = h.shape
    proj = h @ p['w_in']
    q = proj[..., :ATT_WIDTH].reshape(B, T, N_ATT_HEADS, HEAD_DIM)
    k = proj[..., ATT_WIDTH:2 * ATT_WIDTH].reshape(B, T, N_ATT_HEADS, HEAD_DIM)
    v = proj[..., 2 * ATT_WIDTH:3 * ATT_WIDTH].reshape(B, T, N_ATT_HEADS, HEAD_DIM)
    xl = proj[..., 3 * ATT_WIDTH:3 * ATT_WIDTH + LRU_WIDTH]
    gate = proj[..., 3 * ATT_WIDTH + LRU_WIDTH:]
    o_att = attend(q, k, v).reshape(B, T, ATT_WIDTH)
    y_lru, conv_new, h_new = _rg_lru_branch(xl, gate, conv_prefix, h0, p['conv_w'], p['conv_b'],
                                            p['w_a'], p['b_a'], p['w_x'], p['b_x'], p['lam'])
    groups = jnp.concatenate([_rms_norm(o_att, p['g_att_out']), _rms_norm(y_lru, p['g_lru_out'])], axis=-1)
    x = x + groups @ p['w_out']
    x = x + 0.5 * _swiglu(_rms_norm(x, p['g_ffn2']), p['w2_gate'], p['w2_up'], p['w2_down'])
    return x, k, v, conv_new, h_new


def setup_inputs(seed: int = 0) -> dict:
    key = jax.random.key(seed)
    ks = jax.random.split(key, 32)
    f32 = jnp.float32

    def nrm(k, shape, scale):
        return scale * jax.random.normal(k, shape, f32)

    def gain(k, shape):
        return 1.0 + 0.05 * jax.random.normal(k, shape, f32)

    L = DEPTH
    att_buf = min(MAX_WINDOW, PAST_LEN)
    a0 = jax.random.uniform(ks[16], (L, LRU_WIDTH), f32, minval=0.9, maxval=0.999)
    s0 = a0 ** (1.0 / LRU_C)
    lam = jnp.log(s0) - jnp.log1p(-s0)
    return {
        'x_prompt': nrm(ks[0], (BATCH, SEQ, D_MODEL), 1.0),
        'x_sample': nrm(ks[1], (DEC_BATCH, DEC_SEQ, D_MODEL), 1.0),
        'cache_k': nrm(ks[2], (L, DEC_BATCH, att_buf, N_ATT_HEADS, HEAD_DIM), 1.0),
        'cache_v': nrm(ks[3], (L, DEC_BATCH, att_buf, N_ATT_HEADS, HEAD_DIM), 1.0),
        'state_conv': nrm(ks[4], (L, DEC_BATCH, CONV_WIDTH - 1, LRU_WIDTH), 1.0),
        'state_h': nrm(ks[5], (L, DEC_BATCH, LRU_WIDTH), 0.5),
        'g_ffn1': gain(ks[6], (L, D_MODEL)),
        'w1_gate': nrm(ks[7], (L, D_MODEL, D_FF), D_MODEL ** -0.5),
        'w1_up': nrm(ks[8], (L, D_MODEL, D_FF), D_MODEL ** -0.5),
        'w1_down': nrm(ks[9], (L, D_FF, D_MODEL), D_FF ** -0.5),
        'g_mix': gain(ks[10], (L, D_MODEL)),
        'w_in': nrm(ks[11], (L, D_MODEL, IN_WIDTH), D_MODEL ** -0.5),
        'conv_w': nrm(ks[12], (L, CONV_WIDTH, LRU_WIDTH), CONV_WIDTH ** -0.5),
        'conv_b': nrm(ks[13], (L, LRU_WIDTH), 0.01),
        'w_a': nrm(ks[14], (L, LRU_BLOCKS, LRU_BLOCK_W, LRU_BLOCK_W), LRU_BLOCK_W ** -0.5),
        'b_a': nrm(ks[15], (L, LRU_WIDTH), 0.01),
        'w_x': nrm(ks[17], (L, LRU_BLOCKS, LRU_BLOCK_W, LRU_BLOCK_W), LRU_BLOCK_W ** -0.5),
        'b_x': nrm(ks[18], (L, LRU_WIDTH), 0.01),
        'lam': lam,
        'rel_bias': nrm(ks[19], (N_BUCKETS, N_ATT_HEADS), 0.3),
        'g_att_out': gain(ks[20], (L, ATT_WIDTH)),
        'g_lru_out': gain(ks[21], (L, LRU_WIDTH)),
        'w_out': nrm(ks[22], (L, MIX_WIDTH, D_MODEL), MIX_WIDTH ** -0.5),
        'g_ffn2': gain(ks[23], (L, D_MODEL)),
        'w2_gate': nrm(ks[24], (L, D_MODEL, D_FF), D_MODEL ** -0.5),
        'w2_up': nrm(ks[25], (L, D_MODEL, D_FF), D_MODEL ** -0.5),
        'w2_down': nrm(ks[26], (L, D_FF, D_MODEL), D_FF ** -0.5),
        'g_final': gain(ks[27], (D_MODEL,)),
    }


def reference(x_prompt, x_sample, cache_k, cache_v, state_conv, state_h,
              g_ffn1, w1_gate, w1_up, w1_down, g_mix, w_in, conv_w, conv_b,
              w_a, b_a, w_x, b_x, lam, rel_bias, g_att_out, g_lru_out, w_out,
              g_ffn2, w2_gate, w2_up, w2_down, g_final):
    keep = min(MAX_WINDOW, x_prompt.shape[1])
    xp, xs = x_prompt, x_sample
    kp, vp, cp, hp = [], [], [], []
    ksm, vsm, csm, hsm = [], [], [], []
    for l in range(DEPTH):
        p = {'g_ffn1': g_ffn1[l], 'w1_gate': w1_gate[l], 'w1_up': w1_up[l], 'w1_down': w1_down[l],
             'g_mix': g_mix[l], 'w_in': w_in[l], 'conv_w': conv_w[l], 'conv_b': conv_b[l],
             'w_a': w_a[l], 'b_a': b_a[l], 'w_x': w_x[l], 'b_x': b_x[l], 'lam': lam[l],
             'g_att_out': g_att_out[l], 'g_lru_out': g_lru_out[l], 'w_out': w_out[l],
             'g_ffn2': g_ffn2[l], 'w2_gate': w2_gate[l], 'w2_up': w2_up[l], 'w2_down': w2_down[l]}
        zero_conv = jnp.zeros((xp.shape[0], CONV_WIDTH - 1, LRU_WIDTH), xp.dtype)
        zero_h = jnp.zeros((xp.shape[0], LRU_WIDTH), jnp.float32)
        xp, k_l, v_l, c_l, h_l = _layer(xp, p, functools.partial(_attend_prompt, rel_bias=rel_bias),
                                        zero_conv, zero_h)
        kp.append(k_l[:, -keep:])
        vp.append(v_l[:, -keep:])
        cp.append(c_l)
        hp.append(h_l)
        attend_s = functools.partial(_attend_sample, k_past=cache_k[l], v_past=cache_v[l], rel_bias=rel_bias)
        xs, k_l, v_l, c_l, h_l = _layer(xs, p, attend_s, state_conv[l], state_h[l])
        ksm.append(k_l)
        vsm.append(v_l)
        csm.append(c_l)
        hsm.append(h_l)
    y_prompt = _rms_norm(xp, g_final)
    y_sample = _rms_norm(xs, g_final)
    return (y_prompt, y_sample, jnp.stack(kp), jnp.stack(vp), jnp.stack(cp), jnp.stack(hp),
            jnp.stack(ksm), jnp.stack(vsm), jnp.stack(csm), jnp.stack(hsm))
```

```python
import math
from contextlib import ExitStack

import numpy as np
import concourse.bass as bass
import concourse.mybir as mybir
from concourse.bass_utils import run_bass_kernel_spmd

F32 = mybir.dt.float32
BF16 = mybir.dt.bfloat16
ALU = mybir.AluOpType
AF = mybir.ActivationFunctionType
AX = mybir.AxisListType

NCORES = 8
D = 2048
KD = 16
DFF = 5632
ATTW = 1024
LRUW = 1024
NH = 8
INW = 5120
SEQ = 2048
TP = 512
NTILE = SEQ // TP
EPS = 1e-6
ATT_SCALE = 1.0 / math.sqrt(128.0)
NEG = -30000.0
PATTERNS = ((128, 1), (512, 4), (2048, 16))
QUARTERS = ((0, 6), (6, 12), (12, 18), (18, 22))
NSLOT = 3

V_GF1, V_GMIX, V_GF2, V_GFIN = 0, 16, 32, 48
V_GATT, V_GLRU = 64, 72
V_CW = 80
V_CB, V_BA, V_BX, V_LAM = 112, 120, 128, 136
V_ONE = 144
NV = 145


class Buf:
    __slots__ = ("name", "w", "r", "excl", "rng")

    def __init__(self, name, excl=False, rng=None):
        self.name = name
        self.w = {}
        self.r = {}
        self.excl = excl
        self.rng = rng


class Prog:
    ENGS = ("pe", "act", "dve", "pool", "sp")

    def __init__(self):
        self.ops = {e: [] for e in self.ENGS}
        self.tick = {e: 0 for e in self.ENGS}
        self.seen = {e: {} for e in self.ENGS}
        self.chan = {}
        self.semnames = set()
        self.arena = []
        self.retired = []

    def buf(self, name, excl=False, rng=None):
        b = Buf(name, excl, rng)
        if rng is not None:
            self.arena.append(b)
            for rg, toks in self.retired:
                if rg[0] < rng[1] and rng[0] < rg[1]:
                    self._merge(b.w, toks)
        return b

    def retire(self, b):
        self.arena.remove(b)
        toks = {}
        self._merge(toks, b.w)
        self._merge(toks, b.r)
        self.retired.append((b.rng, toks))

    def _expand(self, bufs):
        out = []
        for b in bufs:
            out.append(b)
            if b.rng is not None:
                for o in self.arena:
                    if o is not b and o.rng[0] < b.rng[1] and b.rng[0] < o.rng[1]:
                        out.append(o)
        return out

    def _record(self, eng, reads, writes, tok, fn, inc):
        reads = self._expand(reads)
        writes = self._expand(writes)
        need = {}

        def add(d):
            for s, v in d.items():
                if need.get(s, 0) < v:
                    need[s] = v

        for b in reads:
            add(b.w)
            if b.excl:
                add(b.r)
        for b in writes:
            add(b.w)
            add(b.r)
        waits = []
        seen = self.seen[eng]
        for s, v in need.items():
            if eng == "pe" and s == "t_pe":
                continue
            if seen.get(s, 0) >= v:
                continue
            seen[s] = v
            waits.append((s, v))
        self.ops[eng].append((waits, fn, inc))
        s, v = tok
        for b in reads:
            if b.excl:
                b.w = {s: v}
                b.r = {}
            else:
                if b.r.get(s, 0) < v:
                    b.r[s] = v
        for b in writes:
            b.w = {s: v}
            b.r = {}
        return tok

    def op(self, eng, fn, reads=(), writes=()):
        self.tick[eng] += 1
        s = "t_" + eng
        self.semnames.add(s)
        tok = (s, self.tick[eng])
        return self._record(eng, list(reads), list(writes), tok, fn, (s, 1))

    def dma(self, queue, chan, out, in_, reads=(), writes=()):
        self.chan[chan] = self.chan.get(chan, 0) + 16
        self.semnames.add(chan)
        tok = (chan, self.chan[chan])

        def fn(e, out=out, in_=in_):
            return e.dma_start(out=out, in_=in_)

        return self._record(queue, list(reads), list(writes), tok, fn, (chan, 16))

    @staticmethod
    def _merge(dst, src):
        for k, v in src.items():
            if dst.get(k, 0) < v:
                dst[k] = v

    def split(self, parent, subs):
        for sb_ in subs:
            sb_.w = {}
            sb_.r = {}
            self._merge(sb_.w, parent.w)
            self._merge(sb_.w, parent.r)

    def join(self, parent, subs):
        for sb_ in subs:
            self._merge(parent.r, sb_.w)
            self._merge(parent.r, sb_.r)

    def wait_all(self, eng, toks):
        need = {}
        for s, v in toks:
            if need.get(s, 0) < v:
                need[s] = v
        waits = [(s, v) for s, v in need.items() if self.seen[eng].get(s, 0) < v]
        for s, v in waits:
            self.seen[eng][s] = v
        self.ops[eng].append((waits, None, None))

    def emit(self, nc, es):
        sems = {n: es.enter_context(nc.semaphore(n)) for n in sorted(self.semnames)}
        block = es.enter_context(nc.Block())

        def run(eng_name):
            def body(e):
                for waits, fn, inc in self.ops[eng_name]:
                    for s, v in waits:
                        e.wait_ge(sems[s], v)
                    if fn is not None:
                        ins = fn(e)
                        ins.then_inc(sems[inc[0]], inc[1])
            return body

        block.tensor(run("pe"))
        block.scalar(run("act"))
        block.vector(run("dve"))
        block.gpsimd(run("pool"))
        block.sync(run("sp"))


def rel_bucket_np(dist):
    max_exact = 16
    df = np.maximum(dist, 1).astype(np.float32)
    large = max_exact + (np.log(df / np.float32(max_exact)) / np.float32(math.log(2048 / max_exact))
                         * np.float32(32 - max_exact)).astype(np.int32)
    return np.where(dist < max_exact, dist, np.minimum(large, 31))


def build(cfg):
    n_seq = cfg.get("n_seq", 2)
    n_tiles = cfg.get("n_tiles", NTILE)
    do_sample = cfg.get("do_sample", True)
    do_convert = cfg.get("do_convert", True)

    nc = bass.Bass("TRN2", target_bir_lowering=False)
    es = ExitStack()
    P = Prog()

    def din(name, shape, dt=F32):
        return nc.dram_tensor(name, list(shape), dt, kind="ExternalInput").ap()

    def dout(name, shape):
        return nc.dram_tensor(name, list(shape), F32, kind="ExternalOutput").ap()

    def dscr(name, shape, dt):
        return nc.dram_tensor(name, list(shape), dt).ap()

    xp = din("xp", [2, SEQ, D])
    xs = din("xs", [16, D])
    ck = din("ck", [4, 2048, 1024])
    cv = din("cv", [4, 2048, 1024])
    sconv = din("sconv", [128, 8, 4, 3])
    sh0 = din("sh0", [128, 8, 4])
    wsrc = {
        "w1g": din("w1g", [D, DFF]), "w1u": din("w1u", [D, DFF]), "w1d": din("w1d", [DFF, D]),
        "win": din("win", [D, INW]), "wout": din("wout", [D, D]),
        "w2g": din("w2g", [D, DFF]), "w2u": din("w2u", [D, DFF]), "w2d": din("w2d", [DFF, D]),
    }
    vecs_d = din("vecs", [128, NV])
    wa_d = din("wa", [8, 128, 128])
    wx_d = din("wx", [8, 128, 128])
    btoep_d = din("btoep", [128, 24 * 256])
    sbias_d = din("sbias", [128, 24])
    b0_d = din("b0", [16, 8])
    ident_d = din("ident", [128, 128])
    selm_d = din("selm", [16, 16 * 128])
    selc_d = din("selc", [128, 16 * 16])

    yp = dout("yp", [2, SEQ, D])
    ys = dout("ys", [16, D])
    kp = dout("kp", [2, SEQ, 1024])
    vp = dout("vp", [2, SEQ, 1024])
    cpo = dout("cpo", [2, 3, 1024])
    hpo = dout("hpo", [2, 1024])
    ksm = dout("ksm", [16, 1024])
    vsm = dout("vsm", [16, 1024])
    cso = dout("cso", [4, 3, 1024])
    hso = dout("hso", [4, 1024])

    NSLAB = 180
    wscr = dscr("wscr", [NSLAB, 128, 4096], BF16)
    scrB = [P.buf(f"scr{n}") for n in range(NSLAB)]
    pass_state = {"first": True, "ctr": 0}

    def sb(name, shape, dt):
        return es.enter_context(nc.sbuf_tensor(name, list(shape), dt))

    x_t = sb("x", [128, KD, TP], F32)
    h_t = sb("h", [128, KD, TP], BF16)
    mid_t = sb("mid", [128, 12, TP], BF16)
    grp_t = sb("grp", [128, KD, TP], BF16)
    kt_t = sb("kt", [128, NH, SEQ], BF16)
    qt_t = sb("qt", [128, NH, TP], BF16)
    slabs = [sb(f"slab{i}", [128, 16, 256], BF16) for i in range(NSLOT)]
    masks = sb("masks", [128, 24, 256], BF16)
    stg = [sb(f"stg{i}", [128, 1024], F32) for i in range(2)]
    kvst = [sb(f"kvst{i}", [128, 4, 256], F32) for i in range(2)]
    vecs = sb("vecs_sb", [128, NV], F32)
    nsp = sb("nsp", [128, 8], F32)
    hnsp = sb("hnsp", [128, 8], F32)
    hb = sb("hb", [128, 16], F32)
    wa_t = sb("wa_sb", [128, 8, 128], BF16)
    wx_t = sb("wx_sb", [128, 8, 128], BF16)
    ident = sb("ident_sb", [128, 128], F32)
    ones_bf = sb("ones_bf", [128, 128], BF16)
    sq_t = [sb(f"sq{i}", [128, TP], BF16) for i in range(2)]
    sil_t = [sb(f"sil{i}", [128, TP], BF16) for i in range(2)]
    rs_t = sb("rs", [128, TP], F32)
    rs2_t = sb("rs2", [128, TP], F32)
    xprev = sb("xprev", [128, 8, 3], F32)
    hprev = sb("hprev", [128, 8], F32)
    sconv_sb = sb("sconv_sb", [128, 96], F32)
    sh_sb = sb("sh_sb", [128, 32], F32)
    AR = 6528
    arena = sb("arena", [128, AR], F32)

    ps = [es.enter_context(nc.psum_tensor(f"ps{i}", [128, 512], F32)) for i in range(8)]
    psB = [P.buf(f"ps{i}", excl=True) for i in range(8)]

    xB = [P.buf(f"x{k}") for k in range(KD)]
    hB = [P.buf(f"h{k}") for k in range(KD)]
    midB = [P.buf(f"mid{k}") for k in range(12)]
    grpB = [P.buf(f"grp{k}") for k in range(KD)]
    ktB = [P.buf(f"kt{k}") for k in range(NH)]
    qtB = [P.buf(f"qt{k}") for k in range(NH)]
    slabB = [P.buf(f"slab{k}") for k in range(NSLOT)]
    stgB = [P.buf(f"stg{k}") for k in range(2)]
    kvstB = [P.buf(f"kvst{k}") for k in range(2)]
    sqB = [P.buf(f"sq{k}") for k in range(2)]
    silB = [P.buf(f"sil{k}") for k in range(2)]
    rsB = P.buf("rs")
    rs2B = P.buf("rs2")
    constB = P.buf("const")
    masksB = P.buf("masks")
    xprevB = [P.buf(f"xprev{c}") for c in range(8)]
    hprevB = [P.buf(f"hprev{c}") for c in range(8)]
    scB = [P.buf(f"sc{c}") for c in range(8)]
    shB = [P.buf(f"sh{c}") for c in range(8)]
    vpB = {}
    kpB = {}

    ar_cache = {}

    def ar(name, off, n, dt=F32):
        if dt == F32:
            ap = arena[:, off:off + n]
            w = n
        else:
            w = (n + 1) // 2
            ap = arena[:, off:off + w].bitcast(BF16)[:, 0:n]
        key = (name, off, off + w)
        if key not in ar_cache:
            ar_cache[key] = P.buf(name, rng=(off, off + w))
        return ap, ar_cache[key]

    lt_names = ("xlc0", "gl0", "u0", "ubf0", "xlc1", "gl1", "u1", "ubf1", "r", "i", "a", "t1", "hs", "y")
    lt = {n: P.buf("lt_" + n) for n in lt_names}
    lt_parents = {"xlc0": [stgB[0]], "gl0": [stgB[1]], "u0": [stgB[1]], "ubf0": [silB[0]],
                  "xlc1": [midB[4], midB[5], midB[6]], "gl1": [midB[6], midB[7], midB[8]], "u1": [kvstB[1]], "ubf1": [silB[1]],
                  "r": [kvstB[0]], "i": [kvstB[0]], "a": [kvstB[1]], "t1": [midB[8], midB[9], midB[10]],
                  "hs": [midB[0], midB[1]], "y": [midB[2], midB[3]]}

    def lt_split():
        for n, b in lt.items():
            b.w = {}
            b.r = {}
            for par in lt_parents[n]:
                P._merge(b.w, par.w)
                P._merge(b.w, par.r)

    def lt_join():
        for n, b in lt.items():
            for par in lt_parents[n]:
                P._merge(par.r, b.w)
                P._merge(par.r, b.r)

    def interleave(*gens):
        gens = list(gens)
        while gens:
            for g in list(gens):
                try:
                    next(g)
                except StopIteration:
                    gens.remove(g)

    out_toks = []
    rr = {"slab": 0, "stg": 0, "kvst": 0, "sq": 0, "sil": 0}

    def nxt(key, n):
        v = rr[key]
        rr[key] = (v + 1) % n
        return v

    cl = []
    cl.append(P.dma("sp", "const", vecs[:], vecs_d[:, :], writes=[constB]))
    cl.append(P.dma("sp", "const", ident[:], ident_d[:, :], writes=[constB]))
    btB = P.buf("bt", rng=(0, 3072))
    cl.append(P.dma("pool", "constp", wa_t[:], wa_d.rearrange("n c d -> c n d"), writes=[constB]))
    cl.append(P.dma("pool", "constp", wx_t[:], wx_d.rearrange("n c d -> c n d"), writes=[constB]))
    constB.w = {"const": P.chan["const"], "constp": P.chan["constp"]}
    P.op("dve", lambda e: e.memset(ones_bf[:], 1.0), writes=[constB])
    constB.w["const"] = P.chan["const"]
    constB.w["constp"] = P.chan["constp"]
    for hf in range(2):
        P.dma("sp", "bt", arena[:, 0:3072], btoep_d[:, hf * 3072:(hf + 1) * 3072], writes=[btB])
        P.op("act", lambda e, hf=hf: e.activation(masks[:, hf * 12:(hf + 1) * 12, :].rearrange("p a b -> p (a b)"),
                                                  arena[:, 0:3072], AF.Exp),
             reads=[btB], writes=[masksB])
    tA, tAB = ar("tA", 3072, 8)
    tZ, tZB = ar("tZ", 3080, 8)
    tZ2, tZ2B = ar("tZ2", 3088, 8)
    tS, tSB = ar("tS", 3096, 8)
    lam_ap = vecs[:, V_LAM:V_LAM + 8]
    P.op("act", lambda e: e.activation(tA, lam_ap, AF.Exp, scale=-1.0), reads=[constB], writes=[tAB])
    P.op("dve", lambda e: e.tensor_scalar(tZ, tA, 2.0, None, ALU.add), reads=[tAB], writes=[tZB])
    P.op("dve", lambda e: e.reciprocal(tZ, tZ), reads=[tZB], writes=[tZB])
    P.op("dve", lambda e: e.tensor_tensor(tZ, tZ, tA, ALU.mult), reads=[tZB, tAB], writes=[tZB])
    P.op("dve", lambda e: e.tensor_tensor(tZ2, tZ, tZ, ALU.mult), reads=[tZB], writes=[tZ2B])
    P.op("dve", lambda e: e.tensor_scalar(tS, tZ2, 1.0 / 7.0, 1.0 / 5.0, ALU.mult, ALU.add), reads=[tZ2B], writes=[tSB])
    P.op("dve", lambda e: e.tensor_tensor(tS, tS, tZ2, ALU.mult), reads=[tSB, tZ2B], writes=[tSB])
    P.op("dve", lambda e: e.tensor_scalar(tS, tS, 1.0 / 3.0, None, ALU.add), reads=[tSB], writes=[tSB])
    P.op("dve", lambda e: e.tensor_tensor(tS, tS, tZ2, ALU.mult), reads=[tSB, tZ2B], writes=[tSB])
    P.op("dve", lambda e: e.tensor_scalar(tS, tS, 1.0, None, ALU.add), reads=[tSB], writes=[tSB])
    P.op("dve", lambda e: e.tensor_tensor(tS, tS, tZ, ALU.mult), reads=[tSB, tZB], writes=[tSB])
    nspB = P.buf("nsp")
    P.op("dve", lambda e: e.tensor_scalar(nsp[:], tS, -16.0, None, ALU.mult), reads=[tSB], writes=[nspB])
    P.op("dve", lambda e: e.tensor_scalar(hb[:], vecs[:, V_BA:V_BA + 16], 0.5, None, ALU.mult), reads=[constB], writes=[nspB])
    P.op("dve", lambda e: e.tensor_scalar(hnsp[:], nsp[:], 0.5, None, ALU.mult), reads=[nspB], writes=[nspB])
    for b_ in (btB, tAB, tZB, tZ2B, tSB):
        P.retire(b_)
    ar_cache.clear()

    def load_slab(wname, r0, nk, c0):
        s = nxt("slab", NSLOT)
        idx = pass_state["ctr"]
        pass_state["ctr"] += 1
        dst = slabs[s][:, 0:nk, :]
        scr = wscr[idx][:, 0:nk * 256].rearrange("p (k f) -> p k f", k=nk)
        if pass_state["first"]:
            src = wsrc[wname][r0:r0 + nk * 128, c0:c0 + 256].rearrange("(k p) f -> p k f", p=128)
            P.dma("pool", f"slabp{s}", dst, src, writes=[slabB[s]])
            P.dma("sp", f"wbk{s}", scr, dst, reads=[slabB[s]], writes=[scrB[idx]])
        else:
            P.dma("sp", f"slab{s}", dst, scr, reads=[scrB[idx]], writes=[slabB[s]])
        return s

    def mm_group(mms, reads, writes):
        def fn(e, mms=mms):
            ins = None
            for (o, l, r, st, sp_, skip) in mms:
                if skip:
                    ins = e.matmul(o, l, r, start=st, stop=sp_, skip_group_check=True)
                else:
                    ins = e.matmul(o, l, r, start=st, stop=sp_)
            return ins
        return P.op("pe", fn, reads=reads, writes=writes)

    def rms_stats(src_fn, srcB, nk, T, dn, rs, rsB_, bank):
        for k in range(nk):
            s = nxt("sq", 2)
            P.op("act", lambda e, s=s, k=k: e.activation(sq_t[s][:, :T], src_fn(k), AF.Square),
                 reads=[srcB[k]], writes=[sqB[s]])
            mm_group([(ps[bank][:, :T], ones_bf[:], sq_t[s][:, :T], k == 0, k == nk - 1, False)],
                     reads=[sqB[s], constB], writes=[psB[bank]])
        rstd_from(ps[bank][:, :T], psB[bank], dn, rs[:, :T], rsB_)

    def rstd_from(psap, psb, dn, rsap, rsb):
        P.op("dve", lambda e: e.tensor_scalar(rsap, psap, 1.0 / dn, EPS, ALU.mult, ALU.add), reads=[psb], writes=[rsb])
        P.op("act", lambda e: e.activation(rsap, rsap, AF.Sqrt), reads=[rsb], writes=[rsb])
        P.op("dve", lambda e: e.reciprocal(rsap, rsap), reads=[rsb], writes=[rsb])

    def rms_to_h(gcol, T):
        rms_stats(lambda k: x_t[:, k, :T], xB, KD, T, float(D), rs_t, rsB, 4)
        for k in range(KD):
            P.op("dve", lambda e, k=k: e.scalar_tensor_tensor(
                h_t[:, k, :T], x_t[:, k, :T], vecs[:, gcol + k:gcol + k + 1], rs_t[:, :T], ALU.mult, ALU.mult),
                reads=[xB[k], rsB, constB], writes=[hB[k]])

    bank_rr = [0]

    def next_bank():
        b = bank_rr[0]
        bank_rr[0] = (b + 1) % 4
        return b

    pool_rr = {"att": 0, "lru": 0}

    def pool_bank(which):
        v = pool_rr[which]
        pool_rr[which] = (v + 1) % 2
        return v + (0 if which == "att" else 2)

    def ffn(wg, wu, wd, T):
        for (p0, p1) in QUARTERS:
            for pr in range(p0, p1):
                sg = load_slab(wg, 0, 16, pr * 256)
                su = load_slab(wu, 0, 16, pr * 256)
                for cc in range(2):
                    j = (pr - p0) * 2 + cc
                    bg, bu = next_bank(), next_bank()
                    mm_group([(ps[bg][:, :T], slabs[sg][:, k, cc * 128:(cc + 1) * 128], h_t[:, k, :T], k == 0, k == KD - 1, False)
                              for k in range(KD)], reads=[slabB[sg]] + hB, writes=[psB[bg]])
                    mm_group([(ps[bu][:, :T], slabs[su][:, k, cc * 128:(cc + 1) * 128], h_t[:, k, :T], k == 0, k == KD - 1, False)
                              for k in range(KD)], reads=[slabB[su]] + hB, writes=[psB[bu]])
                    s = nxt("sil", 2)
                    P.op("act", lambda e, s=s, bg=bg: e.activation(sil_t[s][:, :T], ps[bg][:, :T], AF.Silu),
                         reads=[psB[bg]], writes=[silB[s]])
                    P.op("dve", lambda e, s=s, bu=bu, j=j: e.tensor_tensor(mid_t[:, j, :T], sil_t[s][:, :T], ps[bu][:, :T], ALU.mult),
                         reads=[silB[s], psB[bu]], writes=[midB[j]])
            nj = (p1 - p0) * 2
            for dg in range(8):
                sd = load_slab(wd, p0 * 256, nj, dg * 256)
                for cc in range(2):
                    i = dg * 2 + cc
                    b = next_bank()
                    mm_group([(ps[b][:, :T], slabs[sd][:, j, cc * 128:(cc + 1) * 128], mid_t[:, j, :T], j == 0, j == nj - 1, False)
                              for j in range(nj)], reads=[slabB[sd]] + midB[:nj], writes=[psB[b]])
                    P.op("dve", lambda e, b=b, i=i: e.scalar_tensor_tensor(
                        x_t[:, i, :T], ps[b][:, :T], 0.5, x_t[:, i, :T], ALU.mult, ALU.add),
                        reads=[psB[b], xB[i]], writes=[xB[i]])

    def load_x(src2d, T):
        nsub = (T + 127) // 128
        for sub in range(nsub):
            rows = min(128, T - sub * 128)
            for half in range(2):
                s = nxt("stg", 2)
                P.dma("sp", f"stg{s}", stg[s][:rows, :], src2d[sub * 128:sub * 128 + rows, half * 1024:(half + 1) * 1024],
                      writes=[stgB[s]])
                for g in range(2):
                    b = 6 + g
                    k0 = half * 8 + g * 4

                    def fn(e, s=s, g=g, b=b, rows=rows):
                        ins = None
                        for q in range(4):
                            ins = e.transpose(ps[b][:, q * 128:q * 128 + rows],
                                              stg[s][:rows, (g * 4 + q) * 128:(g * 4 + q + 1) * 128], ident[:rows, :rows])
                        return ins
                    P.op("pe", fn, reads=[stgB[s], constB], writes=[psB[b]])
                    P.op("dve", lambda e, b=b, k0=k0, sub=sub, rows=rows: e.tensor_copy(
                        x_t[:, k0:k0 + 4, sub * 128:sub * 128 + rows],
                        ps[b][:, :].rearrange("p (q t) -> p q t", q=4)[:, :, :rows]),
                        reads=[psB[b]], writes=xB[k0:k0 + 4])

    def store_y(dst2d, T):
        nsub = (T + 127) // 128
        for sub in range(nsub):
            rows = min(128, T - sub * 128)
            for half in range(2):
                s = nxt("stg", 2)
                for g in range(2):
                    b = 6 + g
                    k0 = half * 8 + g * 4

                    def fn(e, b=b, k0=k0, sub=sub, rows=rows):
                        ins = None
                        for q in range(4):
                            ins = e.transpose(ps[b][:rows, q * 128:(q + 1) * 128],
                                              x_t[:, k0 + q, sub * 128:sub * 128 + rows], ident[:, :])
                        return ins
                    P.op("pe", fn, reads=xB[k0:k0 + 4] + [constB], writes=[psB[b]])
                    P.op("dve", lambda e, b=b, s=s, g=g, rows=rows: e.tensor_copy(
                        stg[s][:rows, g * 512:(g + 1) * 512], ps[b][:rows, :]),
                        reads=[psB[b]], writes=[stgB[s]])
                t = P.dma("pool", f"stgo{s}", dst2d[sub * 128:sub * 128 + rows, half * 1024:(half + 1) * 1024],
                          stg[s][:rows, :], reads=[stgB[s]])
                out_toks.append(t)

    def lru_chunk(c, T, nseg, L, xl_ps, gate_ps, xl_psB, gate_psB, prefix_fn, h0_fn, hlast_fn, last3_fn):
        W = 3 + L
        b5, b6 = pool_bank("lru"), pool_bank("lru")
        par = c % 2
        midf = mid_t[:, :, :].rearrange("p a b -> p (a b)").bitcast(F32)
        kv0 = kvst[0][:, :, :].rearrange("p a b -> p (a b)")
        kv1 = kvst[1][:, :, :].rearrange("p a b -> p (a b)")
        if par == 0:
            xlc, xlcB = stg[0][:, 0:nseg * W], lt["xlc0"]
            gl, glB = stg[1][:, 0:T], lt["gl0"]
            u, uB = stg[1][:, 512:512 + T], lt["u0"]
            ubf, ubfB = sil_t[0][:, 0:T], lt["ubf0"]
        else:
            xlc, xlcB = midf[:, 1024:1024 + nseg * W], lt["xlc1"]
            gl, glB = midf[:, 1544:1544 + T], lt["gl1"]
            u, uB = kv1[:, 512:512 + T], lt["u1"]
            ubf, ubfB = sil_t[1][:, 0:T], lt["ubf1"]
        r_, rB = kv0[:, 0:T], lt["r"]
        i_, iB = kv0[:, 512:512 + T], lt["i"]
        a_, aB = kv1[:, 0:T], lt["a"]
        t1, t1B = midf[:, 2056:2056 + T], lt["t1"]
        hs, hsB = midf[:, 0:T], lt["hs"]
        y_, yB = midf[:, 512:512 + T], lt["y"]
        xlc3 = xlc.rearrange("p (s w) -> p s w", s=nseg)
        v3 = lambda ap: ap.rearrange("p (s l) -> p s l", s=nseg)
        P.op("dve", lambda e: e.tensor_copy(xlc3[:, :, 3:W], v3(xl_ps)), reads=[xl_psB], writes=[xlcB])
        yield
        P.op("act", lambda e: e.activation(gl, gate_ps, AF.Gelu_apprx_tanh), reads=[gate_psB], writes=[glB])
        yield
        prefix_fn(c, xlc3, xlcB)
        yield
        cw = lambda j: vecs[:, V_CW + j * 8 + c:V_CW + j * 8 + c + 1]
        cb = vecs[:, V_CB + c:V_CB + c + 1]
        P.op("dve", lambda e: e.tensor_scalar(v3(u), xlc3[:, :, 0:L], cw(0), cb, ALU.mult, ALU.add),
             reads=[xlcB, constB], writes=[uB])
        yield
        for j in range(1, 4):
            P.op("dve", lambda e, j=j: e.scalar_tensor_tensor(v3(u), xlc3[:, :, j:j + L], cw(j), v3(u), ALU.mult, ALU.add),
                 reads=[xlcB, uB, constB], writes=[uB])
            yield
        last3_fn(c, xlc3, xlcB)
        yield
        P.op("act", lambda e: e.activation(ubf, u, AF.Copy), reads=[uB], writes=[ubfB])
        yield
        mm_group([(ps[b5][:, :T], wa_t[:, c, :], ubf, True, True, False)], reads=[ubfB, constB], writes=[psB[b5]])
        yield
        mm_group([(ps[b6][:, :T], wx_t[:, c, :], ubf, True, True, False)], reads=[ubfB, constB], writes=[psB[b6]])
        yield
        P.op("act", lambda e: e.activation(r_, ps[b5][:, :T], AF.Tanh, bias=hb[:, c:c + 1], scale=0.5),
             reads=[psB[b5], nspB], writes=[rB])
        yield
        P.op("act", lambda e: e.activation(i_, ps[b6][:, :T], AF.Tanh, bias=hb[:, 8 + c:8 + c + 1], scale=0.5),
             reads=[psB[b6], nspB], writes=[iB])
        yield
        P.op("act", lambda e: e.activation(a_, r_, AF.Exp, bias=hnsp[:, c:c + 1], scale=hnsp[:, c:c + 1]),
             reads=[rB, nspB], writes=[aB])
        yield
        P.op("act", lambda e: e.activation(t1, r_, AF.Exp, bias=nsp[:, c:c + 1], scale=nsp[:, c:c + 1]),
             reads=[rB, nspB], writes=[t1B])
        yield
        P.op("act", lambda e: e.activation(t1, t1, AF.Sqrt, bias=vecs[:, V_ONE:V_ONE + 1], scale=-1.0),
             reads=[t1B, constB], writes=[t1B])
        yield
        P.op("dve", lambda e: e.scalar_tensor_tensor(i_, i_, 1.0, u, ALU.add, ALU.mult), reads=[iB, uB], writes=[iB])
        yield
        P.op("dve", lambda e: e.scalar_tensor_tensor(i_, i_, 0.5, t1, ALU.mult, ALU.mult), reads=[iB, t1B], writes=[iB])
        yield
        for sg in range(nseg):
            init_ap, initB = h0_fn(c, sg)
            P.op("dve", lambda e, sg=sg, init_ap=init_ap: e.tensor_tensor_scan(
                hs[:, sg * L:(sg + 1) * L], a_[:, sg * L:(sg + 1) * L], i_[:, sg * L:(sg + 1) * L], init_ap, ALU.mult, ALU.add),
                reads=[aB, iB, initB], writes=[hsB])
            yield
        hlast_fn(c, v3(hs), hsB)
        yield
        P.op("dve", lambda e: e.tensor_tensor(y_, hs, gl, ALU.mult), reads=[hsB, glB], writes=[yB])
        yield
        s = nxt("sq", 2)
        P.op("act", lambda e, s=s: e.activation(sq_t[s][:, :T], y_, AF.Square), reads=[yB], writes=[sqB[s]])
        yield
        mm_group([(ps[7][:, :T], ones_bf[:], sq_t[s][:, :T], c == 0, c == 7, False)], reads=[sqB[s], constB], writes=[psB[7]])
        yield
        P.op("dve", lambda e: e.tensor_scalar(grp_t[:, 8 + c, :T], y_, vecs[:, V_GLRU + c:V_GLRU + c + 1], None, ALU.mult),
             reads=[yB, constB], writes=[grpB[8 + c]])
        yield

    def att_head(sl, i, hd):
        pos0 = i * TP
        T = TP
        vs = hd % 2
        voff = 0 if vs == 0 else 1856
        v1, v1B = ar(f"v1_{vs}", voff + 0, 5 * 128, BF16)
        v4, v4B = ar(f"v4_{vs}", voff + 320, 8 * 128, BF16)
        v16, v16B = ar(f"v16_{vs}", voff + 832, 16 * 128, BF16)
        v1 = v1.rearrange("p (c e) -> p c e", c=5)
        v4 = v4.rearrange("p (c r e) -> p c r e", c=2, r=4)
        v16 = v16.rearrange("p (r e) -> p r e", r=16)
        cgi = hd // 2
        hc = slice(hd * 128, (hd + 1) * 128)
        c_lo = 1 if i == 0 else 0
        rd = [vpB[(sl, t, cgi)] for t in range(i + 1)]
        P.dma("pool", f"v1_{vs}", v1[:, c_lo:5, :],
              vp[sl, (4 * i - 1 + c_lo) * 128:(4 * i + 4) * 128, hc].rearrange("(c p) e -> p c e", p=128),
              reads=rd, writes=[v1B])
        yield
        t_lo = 1 if i == 0 else 0
        for cc in range(t_lo, 2):
            tt = i - 1 + cc
            P.dma("pool", f"v4_{vs}", v4[:, cc, :, :], vp[sl, tt * 512:(tt + 1) * 512, hc].rearrange("(m r) e -> m r e", r=4),
                  reads=rd, writes=[v4B])
            yield
        nk = 32 * (i + 1)
        P.dma("pool", f"v16_{vs}", v16[:nk, :, :], vp[sl, 0:512 * (i + 1), hc].rearrange("(m r) e -> m r e", r=16),
              reads=rd, writes=[v16B])
        yield

        E = [ar(f"E{q}", 3712 + q * 256, T, BF16) for q in range(2)]
        PT = [ar(f"PT{q}", 4224 + q * 256, T, BF16) for q in range(5)]
        rc, rcB = ar("rc", 5504, T)
        oh, ohB = ar("oh", 6016, T)
        ecnt = [0]

        K = lambda a, b_, st=1: kt_t[:, hd, a:b_:st]
        Q = lambda a, b_, st=1: qt_t[:, hd, a:b_:st]
        m_own = lambda p: masks[:, p * 8 + hd, 0:128]
        m_prev = lambda p: masks[:, p * 8 + hd, 128:256]
        bc = lambda ap, n: ap.unsqueeze(1).broadcast_to([ap.shape[0], n, ap.shape[1]])
        live = []

        def run_scores(blocks, nrows, mask_ap, pt_idx, nblk, blkw, order_rm, col_from=0):
            b = pool_bank("att")
            mms = [(ps[b][:nrows, q * blkw:(q + 1) * blkw], l, r, True, True, False) for (q, l, r) in blocks]
            mm_group(mms, reads=[ktB[hd], qtB[hd]], writes=[psB[b]])
            yield
            ei = ecnt[0] % 2
            ecnt[0] += 1
            Eap, EB = E[ei]
            c0 = col_from * blkw
            P.op("act", lambda e: e.activation(Eap[:nrows, c0:], ps[b][:nrows, c0:], AF.Exp), reads=[psB[b]], writes=[EB])
            yield
            pt, ptB = PT[pt_idx]
            nb = nblk - col_from
            if order_rm:
                ov = pt[:nrows, :].rearrange("k (m r) -> k r m", r=nblk)[:, col_from:, :]
            else:
                ov = pt[:nrows, c0:].rearrange("k (q m) -> k q m", q=nb)
            iv = Eap[:nrows, c0:].rearrange("k (q m) -> k q m", q=nb)
            mk = mask_ap.unsqueeze(1).broadcast_to([nrows, nb, blkw])
            P.op("dve", lambda e: e.tensor_tensor(ov, iv, mk, ALU.mult), reads=[EB, masksB], writes=[ptB])
            yield
            if col_from > 0:
                P.op("dve", lambda e: e.memset(pt[:nrows, 0:c0], 0.0), writes=[ptB])
                yield
            live.append((pt_idx, nrows))

        yield from run_scores([(qb, K(pos0 + qb * 128, pos0 + (qb + 1) * 128), Q(qb * 128, (qb + 1) * 128)) for qb in range(4)],
                   128, m_own(0), 0, 4, 128, False)
        q_lo = 1 if i == 0 else 0
        yield from run_scores([(qb, K(pos0 + (qb - 1) * 128, pos0 + qb * 128), Q(qb * 128, (qb + 1) * 128)) for qb in range(q_lo, 4)],
                   128, m_prev(0), 1, 4, 128, False, col_from=q_lo)
        yield from run_scores([(r, K(pos0 + r, pos0 + 512, 4), Q(r, 512, 4)) for r in range(4)], 128, m_own(1), 2, 4, 128, True)
        if i > 0:
            yield from run_scores([(r, K(pos0 - 512 + r, pos0, 4), Q(r, 512, 4)) for r in range(4)], 128, m_prev(1), 3, 4, 128, True)
        yield from run_scores([(r, K(r, 512 * (i + 1), 16), Q(r, 512, 16)) for r in range(16)], nk,
                   masks[:nk, 2 * 8 + hd, 32 * i:32 * i + 32], 4, 16, 32, True)

        ptBs = [PT[q][1] for q, _ in live]
        mm_group([(ps[5][:, :T], ones_bf[:nr, :], PT[q][0][:nr, :], n == 0, n == len(live) - 1, n > 0)
                  for n, (q, nr) in enumerate(live)], reads=ptBs + [constB], writes=[psB[5]])
        yield
        pv = []
        for qb in range(4):
            pv.append((ps[6][:, qb * 128:(qb + 1) * 128], v1[:, 1 + qb, :], PT[0][0][:, qb * 128:(qb + 1) * 128]))
        for qb in range(q_lo, 4):
            pv.append((ps[6][:, qb * 128:(qb + 1) * 128], v1[:, qb, :], PT[1][0][:, qb * 128:(qb + 1) * 128]))
        for r in range(4):
            pv.append((ps[6][:, r:512:4], v4[:, 1, r, :], PT[2][0][:, r:512:4]))
        if i > 0:
            for r in range(4):
                pv.append((ps[6][:, r:512:4], v4[:, 0, r, :], PT[3][0][:, r:512:4]))
        for r in range(16):
            pv.append((ps[6][:, r:512:16], v16[:nk, r, :], PT[4][0][:nk, r:512:16]))
        mm_group([(o, l, r, n == 0, n == len(pv) - 1, True) for n, (o, l, r) in enumerate(pv)],
                 reads=ptBs + [v1B, v4B, v16B], writes=[psB[6]])
        yield
        P.op("dve", lambda e: e.reciprocal(rc, ps[5][:, :T]), reads=[psB[5]], writes=[rcB])
        yield
        P.op("dve", lambda e: e.tensor_tensor(oh, rc, ps[6][:, :T], ALU.mult), reads=[rcB, psB[6]], writes=[ohB])
        yield
        yield from att_finish(hd, oh, ohB, T)

    def att_finish(hd, oh, ohB, T):
        s = nxt("sq", 2)
        P.op("act", lambda e, s=s: e.activation(sq_t[s][:, :T], oh, AF.Square), reads=[ohB], writes=[sqB[s]])
        yield
        mm_group([(ps[4][:, :T], ones_bf[:], sq_t[s][:, :T], hd == 0, hd == NH - 1, False)], reads=[sqB[s], constB], writes=[psB[4]])
        yield
        P.op("dve", lambda e: e.tensor_scalar(grp_t[:, hd, :T], oh, vecs[:, V_GATT + hd:V_GATT + hd + 1], None, ALU.mult),
             reads=[ohB, constB], writes=[grpB[hd]])
        yield

    def mix_out(T):
        rstd_from(ps[4][:, :T], psB[4], float(ATTW), rs_t[:, :T], rsB)
        rstd_from(ps[7][:, :T], psB[7], float(LRUW), rs2_t[:, :T], rs2B)
        for k in range(KD):
            rsap, rsb = (rs_t, rsB) if k < 8 else (rs2_t, rs2B)
            P.op("dve", lambda e, k=k, rsap=rsap: e.tensor_tensor(grp_t[:, k, :T], grp_t[:, k, :T], rsap[:, :T], ALU.mult),
                 reads=[grpB[k], rsb], writes=[grpB[k]])
        for dg in range(8):
            so = load_slab("wout", 0, 16, dg * 256)
            for cc in range(2):
                i = dg * 2 + cc
                b = next_bank()
                mm_group([(ps[b][:, :T], slabs[so][:, k, cc * 128:(cc + 1) * 128], grp_t[:, k, :T], k == 0, k == KD - 1, False)
                          for k in range(KD)], reads=[slabB[so]] + grpB, writes=[psB[b]])
                P.op("dve", lambda e, b=b, i=i: e.tensor_tensor(x_t[:, i, :T], ps[b][:, :T], x_t[:, i, :T], ALU.add),
                     reads=[psB[b], xB[i]], writes=[xB[i]])

    def final_norm(T):
        rms_stats(lambda k: x_t[:, k, :T], xB, KD, T, float(D), rs_t, rsB, 4)
        for k in range(KD):
            P.op("dve", lambda e, k=k: e.scalar_tensor_tensor(
                x_t[:, k, :T], x_t[:, k, :T], vecs[:, V_GFIN + k:V_GFIN + k + 1], rs_t[:, :T], ALU.mult, ALU.mult),
                reads=[xB[k], rsB, constB], writes=[xB[k]])

    def tm_proj(s_slab, T, dst, dstB_key, dstB_dict, col0, extra=None):
        nsub = (T + 127) // 128
        ks = nxt("kvst", 2)
        for sub in range(nsub):
            rows = min(128, T - sub * 128)
            b = 5 + (sub % 2)
            mm_group([(ps[b][:rows, 0:256], h_t[:, k, sub * 128:sub * 128 + rows], slabs[s_slab][:, k, :], k == 0, k == KD - 1, False)
                      for k in range(KD)], reads=[slabB[s_slab]] + hB, writes=[psB[b]])
            P.op("dve", lambda e, b=b, ks=ks, sub=sub, rows=rows: e.tensor_copy(kvst[ks][:rows, sub, :], ps[b][:rows, 0:256]),
                 reads=[psB[b]], writes=[kvstB[ks]])
        if extra is not None:
            extra(ks)
        if nsub == 4:
            dview = dst[:, col0:col0 + 256].rearrange("(s p) f -> p s f", p=128)
            sview = kvst[ks][:, :, :]
        else:
            dview = dst[:, col0:col0 + 256]
            sview = kvst[ks][:T, 0, :]
        bkey = P.buf("dram")
        dstB_dict[dstB_key] = bkey
        t = P.dma("pool", f"kvo{ks}", dview, sview, reads=[kvstB[ks]], writes=[bkey])
        out_toks.append(t)

    def prompt_tile(sl, i):
        T = TP
        pos0 = i * TP
        load_x(xp[sl, pos0:pos0 + T, :], T)
        rms_to_h(V_GF1, T)
        ffn("w1g", "w1u", "w1d", T)
        rms_to_h(V_GMIX, T)
        for cg in range(4):
            s = load_slab("win", 0, 16, cg * 256)
            for cc in range(2):
                hd = cg * 2 + cc
                b = next_bank()
                mm_group([(ps[b][:, :T], slabs[s][:, k, cc * 128:(cc + 1) * 128], h_t[:, k, :T], k == 0, k == KD - 1, False)
                          for k in range(KD)], reads=[slabB[s]] + hB, writes=[psB[b]])
                P.op("act", lambda e, b=b, hd=hd: e.activation(qt_t[:, hd, :T], ps[b][:, :T], AF.Copy, scale=ATT_SCALE),
                     reads=[psB[b]], writes=[qtB[hd]])
        for cg in range(4):
            s = load_slab("win", 0, 16, 1024 + cg * 256)
            for cc in range(2):
                hd = cg * 2 + cc
                b = next_bank()
                mm_group([(ps[b][:, :T], slabs[s][:, k, cc * 128:(cc + 1) * 128], h_t[:, k, :T], k == 0, k == KD - 1, False)
                          for k in range(KD)], reads=[slabB[s]] + hB, writes=[psB[b]])
                P.op("act", lambda e, b=b, hd=hd: e.activation(kt_t[:, hd, pos0:pos0 + T], ps[b][:, :T], AF.Copy),
                     reads=[psB[b]], writes=[ktB[hd]])
            tm_proj(s, T, kp[sl, pos0:pos0 + T, :], (sl, i, cg), kpB, cg * 256)
        for cg in range(4):
            s = load_slab("win", 0, 16, 2048 + cg * 256)
            tm_proj(s, T, vp[sl, pos0:pos0 + T, :], (sl, i, cg), vpB, cg * 256)
        def prefix_fn(c, xlc3, xlcB):
            if i == 0:
                P.op("dve", lambda e: e.memset(xlc3[:, :, 0:3], 0.0), writes=[xlcB])
            else:
                P.op("dve", lambda e: e.tensor_copy(xlc3[:, 0, 0:3], xprev[:, c, :]), reads=[xprevB[c]], writes=[xlcB])

        def last3_fn(c, xlc3, xlcB):
            P.op("dve", lambda e: e.tensor_copy(xprev[:, c, :], xlc3[:, 0, T:T + 3]), reads=[xlcB], writes=[xprevB[c]])

        def h0_fn(c, sg):
            if i == 0:
                return 0.0, constB
            return hprev[:, c:c + 1], hprevB[c]

        def hlast_fn(c, hs3, hsB):
            P.op("dve", lambda e: e.tensor_copy(hprev[:, c:c + 1], hs3[:, 0, T - 1:T]), reads=[hsB], writes=[hprevB[c]])

        def lru_phase():
            for c2 in range(4):
                sx = load_slab("win", 0, 16, 3072 + c2 * 256)
                sg_ = load_slab("win", 0, 16, 4096 + c2 * 256)
                yield
                for cc in range(2):
                    c = c2 * 2 + cc
                    bx_, bg_ = pool_bank("lru"), pool_bank("lru")
                    mm_group([(ps[bx_][:, :T], slabs[sx][:, k, cc * 128:(cc + 1) * 128], h_t[:, k, :T], k == 0, k == KD - 1, False)
                              for k in range(KD)], reads=[slabB[sx]] + hB, writes=[psB[bx_]])
                    mm_group([(ps[bg_][:, :T], slabs[sg_][:, k, cc * 128:(cc + 1) * 128], h_t[:, k, :T], k == 0, k == KD - 1, False)
                              for k in range(KD)], reads=[slabB[sg_]] + hB, writes=[psB[bg_]])
                    yield
                    yield from lru_chunk(c, T, 1, T, ps[bx_][:, :T], ps[bg_][:, :T], psB[bx_], psB[bg_],
                                         prefix_fn, h0_fn, hlast_fn, last3_fn)
            if i == n_tiles - 1:
                for c in range(8):
                    t = P.dma("pool", "sto", cpo[sl][:, c * 128:(c + 1) * 128].rearrange("j p -> p j"), xprev[:, c, :],
                              reads=[xprevB[c]])
                    out_toks.append(t)
                t = P.dma("pool", "sto", hpo[sl].rearrange("(c p) -> p c", p=128), hprev[:, :], reads=hprevB)
                out_toks.append(t)

        def att_phase():
            for hd in range(NH):
                yield from att_head(sl, i, hd)

        lt_split()
        interleave(lru_phase(), att_phase())
        lt_join()
        mix_out(T)
        rms_to_h(V_GF2, T)
        ffn("w2g", "w2u", "w2d", T)
        final_norm(T)
        store_y(yp[sl, pos0:pos0 + T, :], T)

    for sl in range(n_seq):
        for i in range(n_tiles):
            pass_state["ctr"] = 0
            prompt_tile(sl, i)
            pass_state["first"] = False

    def sample_tile():
        T = 16
        ktf = kt_t[:, :, :].rearrange("p h s -> p (h s)").bitcast(F32)
        slot = lambda k: ktf[:, k * 1024:(k + 1) * 1024]
        qtm, ktm, vtm, otm = slot(0), slot(1), slot(2), slot(7)
        kblk = [slot(3), slot(4)]
        vblk = [slot(5), slot(6)]
        qtmB, ktmB, vtmB, otmB = ktB[0], ktB[1], ktB[2], ktB[7]
        kblkB = [ktB[3], ktB[4]]
        vblkB = [ktB[5], ktB[6]]
        kblkNB = [P.buf("kblkN0"), P.buf("kblkN1")]
        vblkNB = [P.buf("vblkN0"), P.buf("vblkN1")]
        selm = qt_t[:, :, :].rearrange("p h s -> p (h s)").bitcast(F32)
        P.dma("pool", "smc", selm[:16, :], selm_d[:, :], writes=qtB)
        P.dma("pool", "smc", sconv_sb[:], sconv.rearrange("p c s j -> p (c s j)"), writes=scB)
        P.dma("pool", "smc", sh_sb[:], sh0.rearrange("p c s -> p (c s)"), writes=shB)

        load_x(xs[:, :], T)
        rms_to_h(V_GF1, T)
        ffn("w1g", "w1u", "w1d", T)
        rms_to_h(V_GMIX, T)

        def tm_s(s_slab, dst, dstB_, col0, scale):
            b = 5 + (col0 // 256) % 2
            mm_group([(ps[b][:T, 0:256], h_t[:, k, :T], slabs[s_slab][:, k, :], k == 0, k == KD - 1, False)
                      for k in range(KD)], reads=[slabB[s_slab]] + hB, writes=[psB[b]])
            P.op("act", lambda e: e.activation(dst[:T, col0:col0 + 256], ps[b][:T, 0:256], AF.Copy, scale=scale),
                 reads=[psB[b]], writes=[dstB_])

        for cg in range(4):
            s = load_slab("win", 0, 16, cg * 256)
            tm_s(s, qtm, qtmB, cg * 256, ATT_SCALE)
        for cg in range(4):
            s = load_slab("win", 0, 16, 1024 + cg * 256)
            tm_s(s, ktm, ktmB, cg * 256, 1.0)
        ksmB = P.buf("ksm")
        out_toks.append(P.dma("pool", "smo", ksm[:, :], ktm[:T, :], reads=[ktmB], writes=[ksmB]))
        for cg in range(4):
            s = load_slab("win", 0, 16, 2048 + cg * 256)
            tm_s(s, vtm, vtmB, cg * 256, 1.0)
        vsmB = P.buf("vsm")
        out_toks.append(P.dma("pool", "smo", vsm[:, :], vtm[:T, :], reads=[vtmB], writes=[vsmB]))

        sc4 = sconv_sb[:].rearrange("p (c s j) -> p c s j", c=8, s=4)
        sh3 = sh_sb[:].rearrange("p (c s) -> p c s", c=8)

        def prefix_fn(c, xlc3, xlcB):
            P.op("dve", lambda e: e.tensor_copy(xlc3[:, :, 0:3], sc4[:, c, :, :]), reads=[scB[c]], writes=[xlcB])

        def last3_fn(c, xlc3, xlcB):
            P.op("dve", lambda e: e.tensor_copy(sc4[:, c, :, :], xlc3[:, :, 4:7]), reads=[xlcB], writes=[scB[c]])

        def h0_fn(c, sg):
            return sh_sb[:, c * 4 + sg:c * 4 + sg + 1], shB[c]

        def hlast_fn(c, hs3, hsB):
            P.op("dve", lambda e: e.tensor_copy(sh3[:, c, :], hs3[:, :, 3]), reads=[hsB], writes=[shB[c]])

        lt_split()
        for c2 in range(4):
            sx = load_slab("win", 0, 16, 3072 + c2 * 256)
            sg_ = load_slab("win", 0, 16, 4096 + c2 * 256)
            for cc in range(2):
                c = c2 * 2 + cc
                bx_, bg_ = next_bank(), next_bank()
                mm_group([(ps[bx_][:, :T], slabs[sx][:, k, cc * 128:(cc + 1) * 128], h_t[:, k, :T], k == 0, k == KD - 1, False)
                          for k in range(KD)], reads=[slabB[sx]] + hB, writes=[psB[bx_]])
                mm_group([(ps[bg_][:, :T], slabs[sg_][:, k, cc * 128:(cc + 1) * 128], h_t[:, k, :T], k == 0, k == KD - 1, False)
                          for k in range(KD)], reads=[slabB[sg_]] + hB, writes=[psB[bg_]])
                for _ in lru_chunk(c, T, 4, 4, ps[bx_][:, :T], ps[bg_][:, :T], psB[bx_], psB[bg_],
                                   prefix_fn, h0_fn, hlast_fn, last3_fn):
                    pass
        lt_join()
        for c in range(8):
            out_toks.append(P.dma("pool", "sto", cso[:, :, c * 128:(c + 1) * 128].rearrange("s j p -> p s j"),
                                  sc4[:, c, :, :], reads=[scB[c]]))
            out_toks.append(P.dma("pool", "sto", hso[:, c * 128:(c + 1) * 128].rearrange("s p -> p s"),
                                  sh3[:, c, :], reads=[shB[c]]))

        selc, selcB = ar("selc", 0, 256)
        sbias, sbiasB = ar("sbias", 256, 24)
        b0t, b0B = ar("b0t", 280, 8)
        scs2 = [ar(f"scs{q}", 288 + 8 * q, 8) for q in range(2)]
        pm2 = [ar(f"pm{q}", 304 + 8 * q, 8) for q in range(2)]
        sc16, sc16B = ar("sc16", 320, 8)
        p16, p16B = ar("p16", 328, 8)
        rden, rdenB = ar("rden", 336, 8)
        prod2 = [ar(f"prod{q}", 512 + 1024 * q, 1024) for q in range(2)]
        wv2 = [ar(f"wv{q}", 2560 + 1024 * q, 1024) for q in range(2)]
        wvb2 = [(wv2[q][0].bitcast(BF16)[:, 0:1024], wv2[q][1]) for q in range(2)]
        selcb, selcbB = ar("selcb", 4736, 256, BF16)
        oh_all, oh_allB = ar("oh_all", 4608, 128)
        prod, prodB = prod2[0]
        wv, wvB = wv2[0]
        P.dma("pool", "smc", selc, selc_d[:, :], writes=[selcB])
        P.dma("pool", "smc", sbias, sbias_d[:, :], writes=[sbiasB])
        P.op("act", lambda e: e.activation(selcb, selc, AF.Copy), reads=[selcB], writes=[selcbB])
        P.dma("pool", "smc", b0t[:16, :], b0_d[:, :], writes=[b0B])
        h3 = lambda ap: ap.rearrange("p (h e) -> p h e", h=8)
        first = [True]
        ncomb = 0
        for s_ in range(4):
            for t in range(4):
                tok = s_ * 4 + t
                qb = (0, 1) if tok % 2 == 0 else (5, 6)
                mm_group([(ps[qb[0]][:, :], selm[:16, tok * 128:(tok + 1) * 128], qtm[:16, 0:512], True, True, False),
                          (ps[qb[1]][:, :], selm[:16, tok * 128:(tok + 1) * 128], qtm[:16, 512:1024], True, True, False)],
                         reads=[qtmB] + qtB, writes=[psB[qb[0]], psB[qb[1]]])
                for p in range(3):
                    sl_ = ncomb % 2
                    ncomb += 1
                    prod_, prodB_ = prod2[sl_]
                    wv_, wvB_ = wvb2[sl_]
                    scs, scsB = scs2[sl_]
                    pm, pmB = pm2[sl_]
                    for (blk, blkB, blkNB, cache, newd, newB, ch, qd) in (
                            (kblk[sl_], kblkB[sl_], kblkNB[sl_], ck, ksm, ksmB, f"kb{sl_}", "pool"),
                            (vblk[sl_], vblkB[sl_], vblkNB[sl_], cv, vsm, vsmB, f"vb{sl_}", "act")):
                        if p == 0:
                            P.dma(qd, ch, blk[0:128 - t, :], cache[s_, 1920 + t:2048, :], writes=[blkB])
                            if t > 0:
                                P.dma(qd, ch, blk[128 - t:128, :], newd[s_ * 4:s_ * 4 + t, :], reads=[newB], writes=[blkNB])
                        elif p == 1:
                            P.dma(qd, ch, blk[:, :], cache[s_, 1536 + t:2048:4, :], writes=[blkB, blkNB])
                        else:
                            P.dma(qd, ch, blk[:, :], cache[s_, t:2048:16, :], writes=[blkB, blkNB])
                    for hf in range(2):
                        P.op("dve", lambda e, hf=hf, sl_=sl_, prod_=prod_, qb=qb: e.tensor_tensor(
                            prod_[:, hf * 512:(hf + 1) * 512], kblk[sl_][:, hf * 512:(hf + 1) * 512], ps[qb[hf]][:, :], ALU.mult),
                            reads=[kblkB[sl_], kblkNB[sl_], psB[qb[hf]]], writes=[prodB_])
                    P.op("dve", lambda e, scs=scs, prod_=prod_: e.tensor_reduce(scs, h3(prod_), AX.X, ALU.add),
                         reads=[prodB_], writes=[scsB])
                    P.op("dve", lambda e, p=p, scs=scs: e.tensor_tensor(scs, scs, sbias[:, p * 8:(p + 1) * 8], ALU.add),
                         reads=[scsB, sbiasB], writes=[scsB])
                    P.op("act", lambda e, pm=pm, scs=scs: e.activation(pm, scs, AF.Exp), reads=[scsB], writes=[pmB])
                    P.op("dve", lambda e, sl_=sl_, wv_=wv_, pm=pm: e.tensor_tensor(
                        h3(wv_), h3(vblk[sl_]), pm.unsqueeze(2).broadcast_to([128, 8, 128]), ALU.mult),
                        reads=[vblkB[sl_], vblkNB[sl_], pmB], writes=[wvB_])
                    lhs = selc[:, tok * 16:(tok + 1) * 16]
                    lhsb = selcb[:, tok * 16:(tok + 1) * 16]
                    st = first[0]
                    first[0] = False
                    mm_group([(ps[2][:16, :], lhsb, wv_[:, 0:512], st, False, not st),
                              (ps[3][:16, :], lhsb, wv_[:, 512:1024], st, False, not st),
                              (ps[4][:16, 0:8], lhs, pm, st, False, not st)],
                             reads=[wvB_, pmB, selcB, selcbB], writes=[psB[2], psB[3], psB[4]])
        P.op("dve", lambda e: e.tensor_tensor(prod[:16, :], qtm[:16, :], ktm[:16, :], ALU.mult),
             reads=[qtmB, ktmB], writes=[prodB])
        P.op("dve", lambda e: e.tensor_reduce(sc16[:16, :], h3(prod[:16, :]), AX.X, ALU.add), reads=[prodB], writes=[sc16B])
        P.op("dve", lambda e: e.tensor_tensor(sc16[:16, :], sc16[:16, :], b0t[:16, :], ALU.add), reads=[sc16B, b0B], writes=[sc16B])
        P.op("act", lambda e: e.activation(p16[:16, :], sc16[:16, :], AF.Exp), reads=[sc16B], writes=[p16B])
        P.op("dve", lambda e: e.tensor_scalar(p16[:16, :], p16[:16, :], 3.0, None, ALU.mult), reads=[p16B], writes=[p16B])
        P.op("dve", lambda e: e.tensor_tensor(h3(wv[:16, :]), h3(vtm[:16, :]),
                                              p16[:16, :].unsqueeze(2).broadcast_to([16, 8, 128]), ALU.mult),
             reads=[vtmB, p16B], writes=[wvB])
        P.op("dve", lambda e: e.tensor_tensor(rden[:16, :], ps[4][:16, 0:8], p16[:16, :], ALU.add),
             reads=[psB[4], p16B], writes=[rdenB])
        P.op("dve", lambda e: e.reciprocal(rden[:16, :], rden[:16, :]), reads=[rdenB], writes=[rdenB])
        for hf in range(2):
            P.op("dve", lambda e, hf=hf: e.tensor_tensor(otm[:16, hf * 512:(hf + 1) * 512], ps[2 + hf][:16, :],
                                                         wv[:16, hf * 512:(hf + 1) * 512], ALU.add),
                 reads=[psB[2 + hf], wvB], writes=[otmB])
        P.op("dve", lambda e: e.tensor_tensor(h3(otm[:16, :]), h3(otm[:16, :]),
                                              rden[:16, :].unsqueeze(2).broadcast_to([16, 8, 128]), ALU.mult),
             reads=[otmB, rdenB], writes=[otmB])

        def fn_tr(e):
            ins = None
            for hd in range(NH):
                ins = e.transpose(ps[0][:, hd * 16:(hd + 1) * 16], otm[:16, hd * 128:(hd + 1) * 128], ident[:16, :16])
            return ins
        P.op("pe", fn_tr, reads=[otmB, constB], writes=[psB[0]])
        P.op("dve", lambda e: e.tensor_copy(oh_all, ps[0][:, 0:128]), reads=[psB[0]], writes=[oh_allB])
        for hd in range(NH):
            for _ in att_finish(hd, oh_all[:, hd * 16:(hd + 1) * 16], oh_allB, T):
                pass
        mix_out(T)
        rms_to_h(V_GF2, T)
        ffn("w2g", "w2u", "w2d", T)
        final_norm(T)
        store_y(ys[:, :], T)

    if do_sample:
        pass_state["ctr"] = 0
        sample_tile()

    P.wait_all("sp", out_toks)
    with nc.allow_non_contiguous_dma(reason="small state outputs / strided weight slabs"):
        P.emit(nc, es)
    es.close()
    return nc


def host_tables(rel_bias):
    rb = np.asarray(rel_bias, np.float32)
    btoep = np.full((128, 24, 256), NEG, np.float32)
    kk = np.arange(128)[:, None]
    qq = np.arange(128)[None, :]
    sbias = np.zeros((128, 3, 8), np.float32)
    for p, (w, dil) in enumerate(PATTERNS):
        bidx = rel_bucket_np(dil * np.arange(129, dtype=np.int32))
        j_own = qq - kk
        j_prev = 128 + qq - kk
        for h in range(8):
            f = rb[bidx, h]
            own = np.where(j_own >= 0, f[np.clip(j_own, 0, 128)], NEG)
            prev = np.where(j_prev <= 128, f[np.clip(j_prev, 0, 128)], NEG)
            btoep[:, p * 8 + h, 0:128] = own
            btoep[:, p * 8 + h, 128:256] = prev
            sbias[:, p, h] = f[128 - np.arange(128)]
    b0 = np.tile(rb[0][None, :], (16, 1)).astype(np.float32)
    return btoep.reshape(128, 24 * 256), sbias.reshape(128, 24), b0


def col_layout(v, k):
    return np.ascontiguousarray(np.asarray(v, np.float32).reshape(k, 128).T)


def make_in_maps(inp, ncores=NCORES):
    f32 = lambda a: np.ascontiguousarray(np.asarray(a, np.float32))
    vecs = np.zeros((128, NV), np.float32)
    vecs[:, V_GF1:V_GF1 + 16] = col_layout(inp["g_ffn1"][0], 16)
    vecs[:, V_GMIX:V_GMIX + 16] = col_layout(inp["g_mix"][0], 16)
    vecs[:, V_GF2:V_GF2 + 16] = col_layout(inp["g_ffn2"][0], 16)
    vecs[:, V_GFIN:V_GFIN + 16] = col_layout(inp["g_final"], 16)
    vecs[:, V_GATT:V_GATT + 8] = col_layout(inp["g_att_out"][0], 8)
    vecs[:, V_GLRU:V_GLRU + 8] = col_layout(inp["g_lru_out"][0], 8)
    cw = np.asarray(inp["conv_w"][0], np.float32)
    for j in range(4):
        vecs[:, V_CW + j * 8:V_CW + (j + 1) * 8] = col_layout(cw[j], 8)
    vecs[:, V_CB:V_CB + 8] = col_layout(inp["conv_b"][0], 8)
    vecs[:, V_BA:V_BA + 8] = col_layout(inp["b_a"][0], 8)
    vecs[:, V_BX:V_BX + 8] = col_layout(inp["b_x"][0], 8)
    vecs[:, V_LAM:V_LAM + 8] = col_layout(inp["lam"][0], 8)
    vecs[:, V_ONE] = 1.0
    btoep, sbias, b0 = host_tables(inp["rel_bias"])
    ident = np.eye(128, dtype=np.float32)
    selm = np.zeros((16, 16, 128), np.float32)
    selc = np.zeros((128, 16, 16), np.float32)
    for t in range(16):
        selm[t, t, :] = 1.0
        selc[:, t, t] = 1.0
    shared = {
        "w1g": f32(inp["w1_gate"][0]), "w1u": f32(inp["w1_up"][0]), "w1d": f32(inp["w1_down"][0]),
        "win": f32(inp["w_in"][0]), "wout": f32(inp["w_out"][0]),
        "w2g": f32(inp["w2_gate"][0]), "w2u": f32(inp["w2_up"][0]), "w2d": f32(inp["w2_down"][0]),
        "vecs": vecs, "wa": f32(inp["w_a"][0]), "wx": f32(inp["w_x"][0]),
        "btoep": btoep, "sbias": sbias, "b0": b0, "ident": ident,
        "selm": selm.reshape(16, 16 * 128), "selc": selc.reshape(128, 16 * 16),
    }
    xpr = np.asarray(inp["x_prompt"], np.float32)
    xsm = np.asarray(inp["x_sample"], np.float32)
    ckk = np.asarray(inp["cache_k"], np.float32)[0]
    cvv = np.asarray(inp["cache_v"], np.float32)[0]
    sc = np.asarray(inp["state_conv"], np.float32)[0]
    shh = np.asarray(inp["state_h"], np.float32)[0]
    maps = []
    for c in range(ncores):
        m = dict(shared)
        m["xp"] = np.ascontiguousarray(xpr[2 * c:2 * c + 2])
        m["xs"] = np.ascontiguousarray(xsm[4 * c:4 * c + 4].reshape(16, D))
        m["ck"] = np.ascontiguousarray(ckk[4 * c:4 * c + 4].reshape(4, 2048, 1024))
        m["cv"] = np.ascontiguousarray(cvv[4 * c:4 * c + 4].reshape(4, 2048, 1024))
        m["sconv"] = np.ascontiguousarray(sc[4 * c:4 * c + 4].reshape(4, 3, 8, 128).transpose(3, 2, 0, 1))
        m["sh0"] = np.ascontiguousarray(shh[4 * c:4 * c + 4].reshape(4, 8, 128).transpose(2, 1, 0))
        maps.append(m)
    return maps


_NC_CACHE = {}


def kernel(**inputs):
    cfg = {}
    nc = build(cfg)
    maps = make_in_maps(inputs)
    res = run_bass_kernel_spmd(nc, maps, core_ids=list(range(NCORES)))
    R = res.results
    cat = lambda k: np.concatenate([np.asarray(r[k]) for r in R], axis=0)
    y_prompt = cat("yp")
    y_sample = cat("ys").reshape(32, 4, D)
    k_prompt = cat("kp").reshape(1, 16, SEQ, 8, 128)
    v_prompt = cat("vp").reshape(1, 16, SEQ, 8, 128)
    conv_prompt = cat("cpo").reshape(1, 16, 3, 1024)
    h_prompt = cat("hpo").reshape(1, 16, 1024)
    k_sample = cat("ksm").reshape(1, 32, 4, 8, 128)
    v_sample = cat("vsm").reshape(1, 32, 4, 8, 128)
    conv_sample = cat("cso").reshape(1, 32, 3, 1024)
    h_sample = cat("hso").reshape(1, 32, 1024)
    return tuple(np.ascontiguousarray(a, dtype=np.float32) for a in (
        y_prompt, y_sample, k_prompt, v_prompt, conv_prompt, h_prompt, k_sample, v_sample, conv_sample, h_sample))
```

```python
import math
from contextlib import ExitStack

import numpy as np
import concourse.bass as bass
import concourse.mybir as mybir
from concourse.bass_utils import run_bass_kernel_spmd

F32 = mybir.dt.float32
BF16 = mybir.dt.bfloat16
ALU = mybir.AluOpType
AF = mybir.ActivationFunctionType
AX = mybir.AxisListType

NCORES = 8
D = 2048
KD = 16
DFF = 5632
ATTW = 1024
LRUW = 1024
NH = 8
INW = 5120
SEQ = 2048
TP = 512
NTILE = SEQ // TP
EPS = 1e-6
ATT_SCALE = 1.0 / math.sqrt(128.0)
NEG = -30000.0
PATTERNS = ((128, 1), (512, 4), (2048, 16))
QUARTERS = ((0, 6), (6, 12), (12, 18), (18, 22))
NSLOT = 3

V_GF1, V_GMIX, V_GF2, V_GFIN = 0, 16, 32, 48
V_GATT, V_GLRU = 64, 72
V_CW = 80
V_CB, V_BA, V_BX, V_LAM = 112, 120, 128, 136
V_ONE = 144
NV = 145


class Buf:
    __slots__ = ("name", "w", "r", "excl", "rng")

    def __init__(self, name, excl=False, rng=None):
        self.name = name
        self.w = {}
        self.r = {}
        self.excl = excl
        self.rng = rng


class Prog:
    ENGS = ("pe", "act", "dve", "pool", "sp")

    def __init__(self):
        self.ops = {e: [] for e in self.ENGS}
        self.tick = {e: 0 for e in self.ENGS}
        self.seen = {e: {} for e in self.ENGS}
        self.chan = {}
        self.semnames = set()
        self.arena = []
        self.retired = []

    def buf(self, name, excl=False, rng=None):
        b = Buf(name, excl, rng)
        if rng is not None:
            self.arena.append(b)
            for rg, toks in self.retired:
                if rg[0] < rng[1] and rng[0] < rg[1]:
                    self._merge(b.w, toks)
        return b

    def retire(self, b):
        self.arena.remove(b)
        toks = {}
        self._merge(toks, b.w)
        self._merge(toks, b.r)
        self.retired.append((b.rng, toks))

    def _expand(self, bufs):
        out = []
        for b in bufs:
            out.append(b)
            if b.rng is not None:
                for o in self.arena:
                    if o is not b and o.rng[0] < b.rng[1] and b.rng[0] < o.rng[1]:
                        out.append(o)
        return out

    def _record(self, eng, reads, writes, tok, fn, inc):
        reads = self._expand(reads)
        writes = self._expand(writes)
        need = {}

        def add(d):
            for s, v in d.items():
                if need.get(s, 0) < v:
                    need[s] = v

        for b in reads:
            add(b.w)
            if b.excl:
                add(b.r)
        for b in writes:
            add(b.w)
            add(b.r)
        waits = []
        seen = self.seen[eng]
        for s, v in need.items():
            if eng == "pe" and s == "t_pe":
                continue
            if seen.get(s, 0) >= v:
                continue
            seen[s] = v
            waits.append((s, v))
        self.ops[eng].append((waits, fn, inc))
        s, v = tok
        for b in reads:
            if b.excl:
                b.w = {s: v}
                b.r = {}
            else:
                if b.r.get(s, 0) < v:
                    b.r[s] = v
        for b in writes:
            b.w = {s: v}
            b.r = {}
        return tok

    def op(self, eng, fn, reads=(), writes=()):
        self.tick[eng] += 1
        s = "t_" + eng
        self.semnames.add(s)
        tok = (s, self.tick[eng])
        return self._record(eng, list(reads), list(writes), tok, fn, (s, 1))

    def dma(self, queue, chan, out, in_, reads=(), writes=()):
        self.chan[chan] = self.chan.get(chan, 0) + 16
        self.semnames.add(chan)
        tok = (chan, self.chan[chan])

        def fn(e, out=out, in_=in_):
            return e.dma_start(out=out, in_=in_)

        return self._record(queue, list(reads), list(writes), tok, fn, (chan, 16))

    @staticmethod
    def _merge(dst, src):
        for k, v in src.items():
            if dst.get(k, 0) < v:
                dst[k] = v

    def split(self, parent, subs):
        for sb_ in subs:
            sb_.w = {}
            sb_.r = {}
            self._merge(sb_.w, parent.w)
            self._merge(sb_.w, parent.r)

    def join(self, parent, subs):
        for sb_ in subs:
            self._merge(parent.r, sb_.w)
            self._merge(parent.r, sb_.r)

    def wait_all(self, eng, toks):
        need = {}
        for s, v in toks:
            if need.get(s, 0) < v:
                need[s] = v
        waits = [(s, v) for s, v in need.items() if self.seen[eng].get(s, 0) < v]
        for s, v in waits:
            self.seen[eng][s] = v
        self.ops[eng].append((waits, None, None))

    def emit(self, nc, es):
        sems = {n: es.enter_context(nc.semaphore(n)) for n in sorted(self.semnames)}
        block = es.enter_context(nc.Block())

        def run(eng_name):
            def body(e):
                for waits, fn, inc in self.ops[eng_name]:
                    for s, v in waits:
                        e.wait_ge(sems[s], v)
                    if fn is not None:
                        ins = fn(e)
                        ins.then_inc(sems[inc[0]], inc[1])
            return body

        block.tensor(run("pe"))
        block.scalar(run("act"))
        block.vector(run("dve"))
        block.gpsimd(run("pool"))
        block.sync(run("sp"))


def rel_bucket_np(dist):
    max_exact = 16
    df = np.maximum(dist, 1).astype(np.float32)
    large = max_exact + (np.log(df / np.float32(max_exact)) / np.float32(math.log(2048 / max_exact))
                         * np.float32(32 - max_exact)).astype(np.int32)
    return np.where(dist < max_exact, dist, np.minimum(large, 31))


def build(cfg):
    n_seq = cfg.get("n_seq", 2)
    n_tiles = cfg.get("n_tiles", NTILE)
    do_sample = cfg.get("do_sample", True)
    do_convert = cfg.get("do_convert", True)

    nc = bass.Bass("TRN2", target_bir_lowering=False)
    es = ExitStack()
    P = Prog()

    def din(name, shape, dt=F32):
        return nc.dram_tensor(name, list(shape), dt, kind="ExternalInput").ap()

    def dout(name, shape):
        return nc.dram_tensor(name, list(shape), F32, kind="ExternalOutput").ap()

    def dscr(name, shape, dt):
        return nc.dram_tensor(name, list(shape), dt).ap()

    xp = din("xp", [2, SEQ, D])
    xs = din("xs", [16, D])
    ck = din("ck", [4, 2048, 1024])
    cv = din("cv", [4, 2048, 1024])
    sconv = din("sconv", [128, 8, 4, 3])
    sh0 = din("sh0", [128, 8, 4])
    wsrc = {
        "w1g": din("w1g", [D, DFF]), "w1u": din("w1u", [D, DFF]), "w1d": din("w1d", [DFF, D]),
        "win": din("win", [D, INW]), "wout": din("wout", [D, D]),
        "w2g": din("w2g", [D, DFF]), "w2u": din("w2u", [D, DFF]), "w2d": din("w2d", [DFF, D]),
    }
    vecs_d = din("vecs", [128, NV])
    wa_d = din("wa", [8, 128, 128])
    wx_d = din("wx", [8, 128, 128])
    btoep_d = din("btoep", [128, 24 * 256])
    sbias_d = din("sbias", [128, 24])
    b0_d = din("b0", [16, 8])
    ident_d = din("ident", [128, 128])
    selm_d = din("selm", [16, 16 * 128])
    selc_d = din("selc", [128, 16 * 16])

    yp = dout("yp", [2, SEQ, D])
    ys = dout("ys", [16, D])
    kp = dout("kp", [2, SEQ, 1024])
    vp = dout("vp", [2, SEQ, 1024])
    cpo = dout("cpo", [2, 3, 1024])
    hpo = dout("hpo", [2, 1024])
    ksm = dout("ksm", [16, 1024])
    vsm = dout("vsm", [16, 1024])
    cso = dout("cso", [4, 3, 1024])
    hso = dout("hso", [4, 1024])

    NSLAB = 180
    wscr = dscr("wscr", [NSLAB, 128, 4096], BF16)
    scrB = [P.buf(f"scr{n}") for n in range(NSLAB)]
    pass_state = {"first": True, "ctr": 0}

    def sb(name, shape, dt):
        return es.enter_context(nc.sbuf_tensor(name, list(shape), dt))

    x_t = sb("x", [128, KD, TP], F32)
    h_t = sb("h", [128, KD, TP], BF16)
    mid_t = sb("mid", [128, 12, TP], BF16)
    grp_t = sb("grp", [128, KD, TP], BF16)
    kt_t = sb("kt", [128, NH, SEQ], BF16)
    qt_t = sb("qt", [128, NH, TP], BF16)
    slabs = [sb(f"slab{i}", [128, 16, 256], BF16) for i in range(NSLOT)]
    masks = sb("masks", [128, 24, 256], BF16)
    stg = [sb(f"stg{i}", [128, 1024], F32) for i in range(2)]
    kvst = [sb(f"kvst{i}", [128, 4, 256], F32) for i in range(2)]
    vecs = sb("vecs_sb", [128, NV], F32)
    nsp = sb("nsp", [128, 8], F32)
    hnsp = sb("hnsp", [128, 8], F32)
    hb = sb("hb", [128, 16], F32)
    wa_t = sb("wa_sb", [128, 8, 128], BF16)
    wx_t = sb("wx_sb", [128, 8, 128], BF16)
    ident = sb("ident_sb", [128, 128], F32)
    ones_bf = sb("ones_bf", [128, 128], BF16)
    ident_bf = sb("ident_bf", [128, 128], BF16)
    sq_t = [sb(f"sq{i}", [128, TP], BF16) for i in range(2)]
    sil_t = [sb(f"sil{i}", [128, TP], BF16) for i in range(2)]
    rs_t = sb("rs", [128, TP], F32)
    rs2_t = sb("rs2", [128, TP], F32)
    xprev = sb("xprev", [128, 8, 3], F32)
    hprev = sb("hprev", [128, 8], F32)
    sconv_sb = sb("sconv_sb", [128, 96], F32)
    sh_sb = sb("sh_sb", [128, 32], F32)
    AR = 6528
    arena = sb("arena", [128, AR], F32)

    ps = [es.enter_context(nc.psum_tensor(f"ps{i}", [128, 512], F32)) for i in range(8)]
    psB = [P.buf(f"ps{i}", excl=True) for i in range(8)]

    xB = [P.buf(f"x{k}") for k in range(KD)]
    hB = [P.buf(f"h{k}") for k in range(KD)]
    midB = [P.buf(f"mid{k}") for k in range(12)]
    grpB = [P.buf(f"grp{k}") for k in range(KD)]
    ktB = [P.buf(f"kt{k}") for k in range(NH)]
    qtB = [P.buf(f"qt{k}") for k in range(NH)]
    slabB = [P.buf(f"slab{k}") for k in range(NSLOT)]
    stgB = [P.buf(f"stg{k}") for k in range(2)]
    kvstB = [P.buf(f"kvst{k}") for k in range(2)]
    sqB = [P.buf(f"sq{k}") for k in range(2)]
    silB = [P.buf(f"sil{k}") for k in range(2)]
    rsB = P.buf("rs")
    rs2B = P.buf("rs2")
    constB = P.buf("const")
    masksB = P.buf("masks")
    xprevB = [P.buf(f"xprev{c}") for c in range(8)]
    hprevB = [P.buf(f"hprev{c}") for c in range(8)]
    scB = [P.buf(f"sc{c}") for c in range(8)]
    shB = [P.buf(f"sh{c}") for c in range(8)]
    vpB = {}
    kpB = {}

    ar_cache = {}

    def ar(name, off, n, dt=F32):
        if dt == F32:
            ap = arena[:, off:off + n]
            w = n
        else:
            w = (n + 1) // 2
            ap = arena[:, off:off + w].bitcast(BF16)[:, 0:n]
        key = (name, off, off + w)
        if key not in ar_cache:
            ar_cache[key] = P.buf(name, rng=(off, off + w))
        return ap, ar_cache[key]

    lt_names = ("xlc0", "gl0", "u0", "ubf0", "xlc1", "gl1", "u1", "ubf1", "r", "i", "a", "t1", "hs", "y")
    lt = {n: P.buf("lt_" + n) for n in lt_names}
    lt_parents = {"xlc0": [stgB[0]], "gl0": [stgB[1]], "u0": [stgB[1]], "ubf0": [silB[0]],
                  "xlc1": [midB[4], midB[5], midB[6]], "gl1": [midB[6], midB[7], midB[8]], "u1": [kvstB[1]], "ubf1": [silB[1]],
                  "r": [kvstB[0]], "i": [kvstB[0]], "a": [kvstB[1]], "t1": [midB[8], midB[9], midB[10]],
                  "hs": [midB[0], midB[1]], "y": [midB[2], midB[3]]}

    def lt_split():
        for n, b in lt.items():
            b.w = {}
            b.r = {}
            for par in lt_parents[n]:
                P._merge(b.w, par.w)
                P._merge(b.w, par.r)

    def lt_join():
        for n, b in lt.items():
            for par in lt_parents[n]:
                P._merge(par.r, b.w)
                P._merge(par.r, b.r)

    def interleave(*gens):
        gens = list(gens)
        while gens:
            for g in list(gens):
                try:
                    next(g)
                except StopIteration:
                    gens.remove(g)

    out_toks = []
    rr = {"slab": 0, "stg": 0, "kvst": 0, "sq": 0, "sil": 0}

    def nxt(key, n):
        v = rr[key]
        rr[key] = (v + 1) % n
        return v

    cl = []
    cl.append(P.dma("sp", "const", vecs[:], vecs_d[:, :], writes=[constB]))
    cl.append(P.dma("sp", "const", ident[:], ident_d[:, :], writes=[constB]))
    btB = P.buf("bt", rng=(0, 3072))
    cl.append(P.dma("pool", "constp", wa_t[:], wa_d.rearrange("n c d -> c n d"), writes=[constB]))
    cl.append(P.dma("pool", "constp", wx_t[:], wx_d.rearrange("n c d -> c n d"), writes=[constB]))
    constB.w = {"const": P.chan["const"], "constp": P.chan["constp"]}
    P.op("dve", lambda e: e.memset(ones_bf[:], 1.0), writes=[constB])
    constB.w["const"] = P.chan["const"]
    constB.w["constp"] = P.chan["constp"]
    P.op("act", lambda e: e.activation(ident_bf[:], ident[:], AF.Copy), reads=[constB], writes=[masksB])
    for hf in range(2):
        P.dma("sp", "bt", arena[:, 0:3072], btoep_d[:, hf * 3072:(hf + 1) * 3072], writes=[btB])
        P.op("act", lambda e, hf=hf: e.activation(masks[:, hf * 12:(hf + 1) * 12, :].rearrange("p a b -> p (a b)"),
                                                  arena[:, 0:3072], AF.Copy),
             reads=[btB], writes=[masksB])
    tA, tAB = ar("tA", 3072, 8)
    tZ, tZB = ar("tZ", 3080, 8)
    tZ2, tZ2B = ar("tZ2", 3088, 8)
    tS, tSB = ar("tS", 3096, 8)
    lam_ap = vecs[:, V_LAM:V_LAM + 8]
    P.op("act", lambda e: e.activation(tA, lam_ap, AF.Exp, scale=-1.0), reads=[constB], writes=[tAB])
    P.op("dve", lambda e: e.tensor_scalar(tZ, tA, 2.0, None, ALU.add), reads=[tAB], writes=[tZB])
    P.op("dve", lambda e: e.reciprocal(tZ, tZ), reads=[tZB], writes=[tZB])
    P.op("dve", lambda e: e.tensor_tensor(tZ, tZ, tA, ALU.mult), reads=[tZB, tAB], writes=[tZB])
    P.op("dve", lambda e: e.tensor_tensor(tZ2, tZ, tZ, ALU.mult), reads=[tZB], writes=[tZ2B])
    P.op("dve", lambda e: e.tensor_scalar(tS, tZ2, 1.0 / 7.0, 1.0 / 5.0, ALU.mult, ALU.add), reads=[tZ2B], writes=[tSB])
    P.op("dve", lambda e: e.tensor_tensor(tS, tS, tZ2, ALU.mult), reads=[tSB, tZ2B], writes=[tSB])
    P.op("dve", lambda e: e.tensor_scalar(tS, tS, 1.0 / 3.0, None, ALU.add), reads=[tSB], writes=[tSB])
    P.op("dve", lambda e: e.tensor_tensor(tS, tS, tZ2, ALU.mult), reads=[tSB, tZ2B], writes=[tSB])
    P.op("dve", lambda e: e.tensor_scalar(tS, tS, 1.0, None, ALU.add), reads=[tSB], writes=[tSB])
    P.op("dve", lambda e: e.tensor_tensor(tS, tS, tZ, ALU.mult), reads=[tSB, tZB], writes=[tSB])
    nspB = P.buf("nsp")
    P.op("dve", lambda e: e.tensor_scalar(nsp[:], tS, -16.0, None, ALU.mult), reads=[tSB], writes=[nspB])
    P.op("dve", lambda e: e.tensor_scalar(hb[:], vecs[:, V_BA:V_BA + 16], 0.5, None, ALU.mult), reads=[constB], writes=[nspB])
    P.op("dve", lambda e: e.tensor_scalar(hnsp[:], nsp[:], 0.5, None, ALU.mult), reads=[nspB], writes=[nspB])
    for b_ in (btB, tAB, tZB, tZ2B, tSB):
        P.retire(b_)
    ar_cache.clear()

    def load_slab(wname, r0, nk, c0):
        s = nxt("slab", NSLOT)
        idx = pass_state["ctr"]
        pass_state["ctr"] += 1
        dst = slabs[s][:, 0:nk, :]
        scr = wscr[idx][:, 0:nk * 256].rearrange("p (k f) -> p k f", k=nk)
        if pass_state["first"]:
            src = wsrc[wname][r0:r0 + nk * 128, c0:c0 + 256].rearrange("(k p) f -> p k f", p=128)
            P.dma("pool", f"slabp{s}", dst, src, writes=[slabB[s]])
            P.dma("sp", f"wbk{s}", scr, dst, reads=[slabB[s]], writes=[scrB[idx]])
        else:
            P.dma("sp", f"slab{s}", dst, scr, reads=[scrB[idx]], writes=[slabB[s]])
        return s

    def mm_group(mms, reads, writes):
        def fn(e, mms=mms):
            ins = None
            for (o, l, r, st, sp_, skip) in mms:
                if skip:
                    ins = e.matmul(o, l, r, start=st, stop=sp_, skip_group_check=True)
                else:
                    ins = e.matmul(o, l, r, start=st, stop=sp_)
            return ins
        return P.op("pe", fn, reads=reads, writes=writes)

    def rms_stats(src_fn, srcB, nk, T, dn, rs, rsB_, bank):
        for k in range(nk):
            s = nxt("sq", 2)
            P.op("act", lambda e, s=s, k=k: e.activation(sq_t[s][:, :T], src_fn(k), AF.Square),
                 reads=[srcB[k]], writes=[sqB[s]])
            mm_group([(ps[bank][:, :T], ones_bf[:], sq_t[s][:, :T], k == 0, k == nk - 1, False)],
                     reads=[sqB[s], constB], writes=[psB[bank]])
        rstd_from(ps[bank][:, :T], psB[bank], dn, rs[:, :T], rsB_)

    def rstd_from(psap, psb, dn, rsap, rsb):
        P.op("dve", lambda e: e.tensor_scalar(rsap, psap, 1.0 / dn, EPS, ALU.mult, ALU.add), reads=[psb], writes=[rsb])
        P.op("act", lambda e: e.activation(rsap, rsap, AF.Sqrt), reads=[rsb], writes=[rsb])
        P.op("dve", lambda e: e.reciprocal(rsap, rsap), reads=[rsb], writes=[rsb])

    def rms_to_h(gcol, T):
        rms_stats(lambda k: x_t[:, k, :T], xB, KD, T, float(D), rs_t, rsB, 4)
        for k in range(KD):
            P.op("dve", lambda e, k=k: e.scalar_tensor_tensor(
                h_t[:, k, :T], x_t[:, k, :T], vecs[:, gcol + k:gcol + k + 1], rs_t[:, :T], ALU.mult, ALU.mult),
                reads=[xB[k], rsB, constB], writes=[hB[k]])

    bank_rr = [0]

    def next_bank():
        b = bank_rr[0]
        bank_rr[0] = (b + 1) % 4
        return b

    pool_rr = {"att": 0, "lru": 0}

    def pool_bank(which):
        v = pool_rr[which]
        pool_rr[which] = (v + 1) % 2
        return v + (0 if which == "att" else 2)

    def ffn(wg, wu, wd, T):
        for (p0, p1) in QUARTERS:
            for pr in range(p0, p1):
                sg = load_slab(wg, 0, 16, pr * 256)
                su = load_slab(wu, 0, 16, pr * 256)
                for cc in range(2):
                    j = (pr - p0) * 2 + cc
                    bg, bu = next_bank(), next_bank()
                    mm_group([(ps[bg][:, :T], slabs[sg][:, k, cc * 128:(cc + 1) * 128], h_t[:, k, :T], k == 0, k == KD - 1, False)
                              for k in range(KD)], reads=[slabB[sg]] + hB, writes=[psB[bg]])
                    mm_group([(ps[bu][:, :T], slabs[su][:, k, cc * 128:(cc + 1) * 128], h_t[:, k, :T], k == 0, k == KD - 1, False)
                              for k in range(KD)], reads=[slabB[su]] + hB, writes=[psB[bu]])
                    s = nxt("sil", 2)
                    P.op("act", lambda e, s=s, bg=bg: e.activation(sil_t[s][:, :T], ps[bg][:, :T], AF.Silu),
                         reads=[psB[bg]], writes=[silB[s]])
                    P.op("dve", lambda e, s=s, bu=bu, j=j: e.tensor_tensor(mid_t[:, j, :T], sil_t[s][:, :T], ps[bu][:, :T], ALU.mult),
                         reads=[silB[s], psB[bu]], writes=[midB[j]])
            nj = (p1 - p0) * 2
            for dg in range(8):
                sd = load_slab(wd, p0 * 256, nj, dg * 256)
                for cc in range(2):
                    i = dg * 2 + cc
                    b = next_bank()
                    mm_group([(ps[b][:, :T], slabs[sd][:, j, cc * 128:(cc + 1) * 128], mid_t[:, j, :T], j == 0, j == nj - 1, False)
                              for j in range(nj)], reads=[slabB[sd]] + midB[:nj], writes=[psB[b]])
                    P.op("dve", lambda e, b=b, i=i: e.scalar_tensor_tensor(
                        x_t[:, i, :T], ps[b][:, :T], 0.5, x_t[:, i, :T], ALU.mult, ALU.add),
                        reads=[psB[b], xB[i]], writes=[xB[i]])

    def load_x(src2d, T):
        nsub = (T + 127) // 128
        for sub in range(nsub):
            rows = min(128, T - sub * 128)
            for half in range(2):
                s = nxt("stg", 2)
                P.dma("sp", f"stg{s}", stg[s][:rows, :], src2d[sub * 128:sub * 128 + rows, half * 1024:(half + 1) * 1024],
                      writes=[stgB[s]])
                for g in range(2):
                    b = 6 + g
                    k0 = half * 8 + g * 4

                    def fn(e, s=s, g=g, b=b, rows=rows):
                        ins = None
                        for q in range(4):
                            ins = e.transpose(ps[b][:, q * 128:q * 128 + rows],
                                              stg[s][:rows, (g * 4 + q) * 128:(g * 4 + q + 1) * 128], ident[:rows, :rows])
                        return ins
                    P.op("pe", fn, reads=[stgB[s], constB], writes=[psB[b]])
                    P.op("dve", lambda e, b=b, k0=k0, sub=sub, rows=rows: e.tensor_copy(
                        x_t[:, k0:k0 + 4, sub * 128:sub * 128 + rows],
                        ps[b][:, :].rearrange("p (q t) -> p q t", q=4)[:, :, :rows]),
                        reads=[psB[b]], writes=xB[k0:k0 + 4])

    def store_y(dst2d, T):
        nsub = (T + 127) // 128
        for sub in range(nsub):
            rows = min(128, T - sub * 128)
            for half in range(2):
                s = nxt("stg", 2)
                for g in range(2):
                    b = 6 + g
                    k0 = half * 8 + g * 4

                    def fn(e, b=b, k0=k0, sub=sub, rows=rows):
                        ins = None
                        for q in range(4):
                            ins = e.transpose(ps[b][:rows, q * 128:(q + 1) * 128],
                                              x_t[:, k0 + q, sub * 128:sub * 128 + rows], ident[:, :])
                        return ins
                    P.op("pe", fn, reads=xB[k0:k0 + 4] + [constB], writes=[psB[b]])
                    P.op("dve", lambda e, b=b, s=s, g=g, rows=rows: e.tensor_copy(
                        stg[s][:rows, g * 512:(g + 1) * 512], ps[b][:rows, :]),
                        reads=[psB[b]], writes=[stgB[s]])
                t = P.dma("pool", f"stgo{s}", dst2d[sub * 128:sub * 128 + rows, half * 1024:(half + 1) * 1024],
                          stg[s][:rows, :], reads=[stgB[s]])
                out_toks.append(t)

    def lru_chunk(c, T, nseg, L, xl_ps, gate_ps, xl_psB, gate_psB, prefix_fn, h0_fn, hlast_fn, last3_fn):
        W = 3 + L
        b5, b6 = pool_bank("lru"), pool_bank("lru")
        par = c % 2
        midf = mid_t[:, :, :].rearrange("p a b -> p (a b)").bitcast(F32)
        kv0 = kvst[0][:, :, :].rearrange("p a b -> p (a b)")
        kv1 = kvst[1][:, :, :].rearrange("p a b -> p (a b)")
        if par == 0:
            xlc, xlcB = stg[0][:, 0:nseg * W], lt["xlc0"]
            gl, glB = stg[1][:, 0:T], lt["gl0"]
            u, uB = stg[1][:, 512:512 + T], lt["u0"]
            ubf, ubfB = sil_t[0][:, 0:T], lt["ubf0"]
        else:
            xlc, xlcB = midf[:, 1024:1024 + nseg * W], lt["xlc1"]
            gl, glB = midf[:, 1544:1544 + T], lt["gl1"]
            u, uB = kv1[:, 512:512 + T], lt["u1"]
            ubf, ubfB = sil_t[1][:, 0:T], lt["ubf1"]
        r_, rB = kv0[:, 0:T], lt["r"]
        i_, iB = kv0[:, 512:512 + T], lt["i"]
        a_, aB = kv1[:, 0:T], lt["a"]
        t1, t1B = midf[:, 2056:2056 + T], lt["t1"]
        hs, hsB = midf[:, 0:T], lt["hs"]
        y_, yB = midf[:, 512:512 + T], lt["y"]
        xlc3 = xlc.rearrange("p (s w) -> p s w", s=nseg)
        v3 = lambda ap: ap.rearrange("p (s l) -> p s l", s=nseg)
        P.op("dve", lambda e: e.tensor_copy(xlc3[:, :, 3:W], v3(xl_ps)), reads=[xl_psB], writes=[xlcB])
        yield
        P.op("act", lambda e: e.activation(gl, gate_ps, AF.Gelu_apprx_tanh), reads=[gate_psB], writes=[glB])
        yield
        prefix_fn(c, xlc3, xlcB)
        yield
        cw = lambda j: vecs[:, V_CW + j * 8 + c:V_CW + j * 8 + c + 1]
        cb = vecs[:, V_CB + c:V_CB + c + 1]
        P.op("dve", lambda e: e.tensor_scalar(v3(u), xlc3[:, :, 0:L], cw(0), cb, ALU.mult, ALU.add),
             reads=[xlcB, constB], writes=[uB])
        yield
        for j in range(1, 4):
            P.op("dve", lambda e, j=j: e.scalar_tensor_tensor(v3(u), xlc3[:, :, j:j + L], cw(j), v3(u), ALU.mult, ALU.add),
                 reads=[xlcB, uB, constB], writes=[uB])
            yield
        last3_fn(c, xlc3, xlcB)
        yield
        P.op("act", lambda e: e.activation(ubf, u, AF.Copy), reads=[uB], writes=[ubfB])
        yield
        mm_group([(ps[b5][:, :T], wa_t[:, c, :], ubf, True, True, False)], reads=[ubfB, constB], writes=[psB[b5]])
        yield
        mm_group([(ps[b6][:, :T], wx_t[:, c, :], ubf, True, True, False)], reads=[ubfB, constB], writes=[psB[b6]])
        yield
        P.op("act", lambda e: e.activation(r_, ps[b5][:, :T], AF.Tanh, bias=hb[:, c:c + 1], scale=0.5),
             reads=[psB[b5], nspB], writes=[rB])
        yield
        P.op("act", lambda e: e.activation(i_, ps[b6][:, :T], AF.Tanh, bias=hb[:, 8 + c:8 + c + 1], scale=0.5),
             reads=[psB[b6], nspB], writes=[iB])
        yield
        P.op("act", lambda e: e.activation(a_, r_, AF.Exp, bias=hnsp[:, c:c + 1], scale=hnsp[:, c:c + 1]),
             reads=[rB, nspB], writes=[aB])
        yield
        P.op("act", lambda e: e.activation(t1, r_, AF.Exp, bias=nsp[:, c:c + 1], scale=nsp[:, c:c + 1]),
             reads=[rB, nspB], writes=[t1B])
        yield
        P.op("act", lambda e: e.activation(t1, t1, AF.Sqrt, bias=vecs[:, V_ONE:V_ONE + 1], scale=-1.0),
             reads=[t1B, constB], writes=[t1B])
        yield
        P.op("dve", lambda e: e.scalar_tensor_tensor(i_, i_, 1.0, u, ALU.add, ALU.mult), reads=[iB, uB], writes=[iB])
        yield
        P.op("dve", lambda e: e.scalar_tensor_tensor(i_, i_, 0.5, t1, ALU.mult, ALU.mult), reads=[iB, t1B], writes=[iB])
        yield
        for sg in range(nseg):
            init_ap, initB = h0_fn(c, sg)
            P.op("dve", lambda e, sg=sg, init_ap=init_ap: e.tensor_tensor_scan(
                hs[:, sg * L:(sg + 1) * L], a_[:, sg * L:(sg + 1) * L], i_[:, sg * L:(sg + 1) * L], init_ap, ALU.mult, ALU.add),
                reads=[aB, iB, initB], writes=[hsB])
            yield
        hlast_fn(c, v3(hs), hsB)
        yield
        P.op("dve", lambda e: e.tensor_tensor(y_, hs, gl, ALU.mult), reads=[hsB, glB], writes=[yB])
        yield
        s = nxt("sq", 2)
        P.op("act", lambda e, s=s: e.activation(sq_t[s][:, :T], y_, AF.Square), reads=[yB], writes=[sqB[s]])
        yield
        mm_group([(ps[7][:, :T], ones_bf[:], sq_t[s][:, :T], c == 0, c == 7, False)], reads=[sqB[s], constB], writes=[psB[7]])
        yield
        P.op("dve", lambda e: e.tensor_scalar(grp_t[:, 8 + c, :T], y_, vecs[:, V_GLRU + c:V_GLRU + c + 1], None, ALU.mult),
             reads=[yB, constB], writes=[grpB[8 + c]])
        yield

    def att_head(sl, i, hd):
        pos0 = i * TP
        T = TP
        vs = hd % 2
        voff = 0 if vs == 0 else 1856
        v1, v1B = ar(f"v1_{vs}", voff + 0, 5 * 128, BF16)
        v4, v4B = ar(f"v4_{vs}", voff + 320, 8 * 128, BF16)
        v16, v16B = ar(f"v16_{vs}", voff + 832, 16 * 128, BF16)
        v1 = v1.rearrange("p (c e) -> p c e", c=5)
        v4 = v4.rearrange("p (c r e) -> p c r e", c=2, r=4)
        v16 = v16.rearrange("p (r e) -> p r e", r=16)
        cgi = hd // 2
        hc = slice(hd * 128, (hd + 1) * 128)
        c_lo = 1 if i == 0 else 0
        rd = [vpB[(sl, t, cgi)] for t in range(i + 1)]
        P.dma("pool", f"v1_{vs}", v1[:, c_lo:5, :],
              vp[sl, (4 * i - 1 + c_lo) * 128:(4 * i + 4) * 128, hc].rearrange("(c p) e -> p c e", p=128),
              reads=rd, writes=[v1B])
        yield
        t_lo = 1 if i == 0 else 0
        for cc in range(t_lo, 2):
            tt = i - 1 + cc
            P.dma("pool", f"v4_{vs}", v4[:, cc, :, :], vp[sl, tt * 512:(tt + 1) * 512, hc].rearrange("(m r) e -> m r e", r=4),
                  reads=rd, writes=[v4B])
            yield
        nk = 32 * (i + 1)
        P.dma("pool", f"v16_{vs}", v16[:nk, :, :], vp[sl, 0:512 * (i + 1), hc].rearrange("(m r) e -> m r e", r=16),
              reads=rd, writes=[v16B])
        yield

        E = [ar(f"E{q}", 3712 + q * 256, T, BF16) for q in range(2)]
        PT = [ar(f"PT{q}", 4224 + q * 256, T, BF16) for q in range(5)]
        rc, rcB = ar("rc", 5504, T)
        oh, ohB = ar("oh", 6016, T)
        ecnt = [0]

        K = lambda a, b_, st=1: kt_t[:, hd, a:b_:st]
        Q = lambda a, b_, st=1: qt_t[:, hd, a:b_:st]
        m_own = lambda p: masks[:, p * 8 + hd, 0:128]
        m_prev = lambda p: masks[:, p * 8 + hd, 128:256]
        bc = lambda ap, n: ap.unsqueeze(1).broadcast_to([ap.shape[0], n, ap.shape[1]])
        live = []

        def run_scores(blocks, nrows, mask_ap, pt_idx, nblk, blkw, order_rm, col_from=0):
            b = pool_bank("att")
            c0 = col_from * blkw
            nb = nblk - col_from
            mk = mask_ap.unsqueeze(1).broadcast_to([nrows, nb, blkw])
            mms = [(ps[b][:nrows, c0:], ident_bf[:nrows, :nrows], mk, True, False, False)]
            mms += [(ps[b][:nrows, q * blkw:(q + 1) * blkw], l, r, False, n == len(blocks) - 1, True)
                    for n, (q, l, r) in enumerate(blocks)]
            mm_group(mms, reads=[ktB[hd], qtB[hd], masksB], writes=[psB[b]])
            yield
            pt, ptB = PT[pt_idx]
            if order_rm:
                ov = pt[:nrows, :].rearrange("k (m r) -> k r m", r=nblk)[:, col_from:, :]
            else:
                ov = pt[:nrows, c0:].rearrange("k (q m) -> k q m", q=nb)
            iv = ps[b][:nrows, c0:].rearrange("k (q m) -> k q m", q=nb)
            P.op("act", lambda e: e.activation(ov, iv, AF.Exp), reads=[psB[b]], writes=[ptB])
            yield
            if col_from > 0:
                P.op("dve", lambda e: e.memset(pt[:nrows, 0:c0], 0.0), writes=[ptB])
                yield
            live.append((pt_idx, nrows))

        yield from run_scores([(qb, K(pos0 + qb * 128, pos0 + (qb + 1) * 128), Q(qb * 128, (qb + 1) * 128)) for qb in range(4)],
                   128, m_own(0), 0, 4, 128, False)
        q_lo = 1 if i == 0 else 0
        yield from run_scores([(qb, K(pos0 + (qb - 1) * 128, pos0 + qb * 128), Q(qb * 128, (qb + 1) * 128)) for qb in range(q_lo, 4)],
                   128, m_prev(0), 1, 4, 128, False, col_from=q_lo)
        yield from run_scores([(r, K(pos0 + r, pos0 + 512, 4), Q(r, 512, 4)) for r in range(4)], 128, m_own(1), 2, 4, 128, True)
        if i > 0:
            yield from run_scores([(r, K(pos0 - 512 + r, pos0, 4), Q(r, 512, 4)) for r in range(4)], 128, m_prev(1), 3, 4, 128, True)
        yield from run_scores([(r, K(r, 512 * (i + 1), 16), Q(r, 512, 16)) for r in range(16)], nk,
                   masks[:nk, 2 * 8 + hd, 32 * i:32 * i + 32], 4, 16, 32, True)

        ptBs = [PT[q][1] for q, _ in live]
        mm_group([(ps[5][:, :T], ones_bf[:nr, :], PT[q][0][:nr, :], n == 0, n == len(live) - 1, n > 0)
                  for n, (q, nr) in enumerate(live)], reads=ptBs + [constB], writes=[psB[5]])
        yield
        pv = []
        for qb in range(4):
            pv.append((ps[6][:, qb * 128:(qb + 1) * 128], v1[:, 1 + qb, :], PT[0][0][:, qb * 128:(qb + 1) * 128]))
        for qb in range(q_lo, 4):
            pv.append((ps[6][:, qb * 128:(qb + 1) * 128], v1[:, qb, :], PT[1][0][:, qb * 128:(qb + 1) * 128]))
        for r in range(4):
            pv.append((ps[6][:, r:512:4], v4[:, 1, r, :], PT[2][0][:, r:512:4]))
        if i > 0:
            for r in range(4):
                pv.append((ps[6][:, r:512:4], v4[:, 0, r, :], PT[3][0][:, r:512:4]))
        for r in range(16):
            pv.append((ps[6][:, r:512:16], v16[:nk, r, :], PT[4][0][:nk, r:512:16]))
        mm_group([(o, l, r, n == 0, n == len(pv) - 1, True) for n, (o, l, r) in enumerate(pv)],
                 reads=ptBs + [v1B, v4B, v16B], writes=[psB[6]])
        yield
        P.op("dve", lambda e: e.reciprocal(rc, ps[5][:, :T]), reads=[psB[5]], writes=[rcB])
        yield
        P.op("dve", lambda e: e.tensor_tensor(oh, rc, ps[6][:, :T], ALU.mult), reads=[rcB, psB[6]], writes=[ohB])
        yield
        yield from att_finish(hd, oh, ohB, T)

    def att_finish(hd, oh, ohB, T):
        s = nxt("sq", 2)
        P.op("act", lambda e, s=s: e.activation(sq_t[s][:, :T], oh, AF.Square), reads=[ohB], writes=[sqB[s]])
        yield
        mm_group([(ps[4][:, :T], ones_bf[:], sq_t[s][:, :T], hd == 0, hd == NH - 1, False)], reads=[sqB[s], constB], writes=[psB[4]])
        yield
        P.op("dve", lambda e: e.tensor_scalar(grp_t[:, hd, :T], oh, vecs[:, V_GATT + hd:V_GATT + hd + 1], None, ALU.mult),
             reads=[ohB, constB], writes=[grpB[hd]])
        yield

    def mix_out(T):
        rstd_from(ps[4][:, :T], psB[4], float(ATTW), rs_t[:, :T], rsB)
        rstd_from(ps[7][:, :T], psB[7], float(LRUW), rs2_t[:, :T], rs2B)
        for k in range(KD):
            rsap, rsb = (rs_t, rsB) if k < 8 else (rs2_t, rs2B)
            P.op("dve", lambda e, k=k, rsap=rsap: e.tensor_tensor(grp_t[:, k, :T], grp_t[:, k, :T], rsap[:, :T], ALU.mult),
                 reads=[grpB[k], rsb], writes=[grpB[k]])
        for dg in range(8):
            so = load_slab("wout", 0, 16, dg * 256)
            for cc in range(2):
                i = dg * 2 + cc
                b = next_bank()
                mm_group([(ps[b][:, :T], slabs[so][:, k, cc * 128:(cc + 1) * 128], grp_t[:, k, :T], k == 0, k == KD - 1, False)
                          for k in range(KD)], reads=[slabB[so]] + grpB, writes=[psB[b]])
                P.op("dve", lambda e, b=b, i=i: e.tensor_tensor(x_t[:, i, :T], ps[b][:, :T], x_t[:, i, :T], ALU.add),
                     reads=[psB[b], xB[i]], writes=[xB[i]])

    def final_norm(T):
        rms_stats(lambda k: x_t[:, k, :T], xB, KD, T, float(D), rs_t, rsB, 4)
        for k in range(KD):
            P.op("dve", lambda e, k=k: e.scalar_tensor_tensor(
                x_t[:, k, :T], x_t[:, k, :T], vecs[:, V_GFIN + k:V_GFIN + k + 1], rs_t[:, :T], ALU.mult, ALU.mult),
                reads=[xB[k], rsB, constB], writes=[xB[k]])

    def tm_proj(s_slab, T, dst, dstB_key, dstB_dict, col0, extra=None):
        nsub = (T + 127) // 128
        ks = nxt("kvst", 2)
        for sub in range(nsub):
            rows = min(128, T - sub * 128)
            b = 5 + (sub % 2)
            mm_group([(ps[b][:rows, 0:256], h_t[:, k, sub * 128:sub * 128 + rows], slabs[s_slab][:, k, :], k == 0, k == KD - 1, False)
                      for k in range(KD)], reads=[slabB[s_slab]] + hB, writes=[psB[b]])
            P.op("dve", lambda e, b=b, ks=ks, sub=sub, rows=rows: e.tensor_copy(kvst[ks][:rows, sub, :], ps[b][:rows, 0:256]),
                 reads=[psB[b]], writes=[kvstB[ks]])
        if extra is not None:
            extra(ks)
        if nsub == 4:
            dview = dst[:, col0:col0 + 256].rearrange("(s p) f -> p s f", p=128)
            sview = kvst[ks][:, :, :]
        else:
            dview = dst[:, col0:col0 + 256]
            sview = kvst[ks][:T, 0, :]
        bkey = P.buf("dram")
        dstB_dict[dstB_key] = bkey
        t = P.dma("pool", f"kvo{ks}", dview, sview, reads=[kvstB[ks]], writes=[bkey])
        out_toks.append(t)

    def prompt_tile(sl, i):
        T = TP
        pos0 = i * TP
        load_x(xp[sl, pos0:pos0 + T, :], T)
        rms_to_h(V_GF1, T)
        ffn("w1g", "w1u", "w1d", T)
        rms_to_h(V_GMIX, T)
        for cg in range(4):
            s = load_slab("win", 0, 16, cg * 256)
            for cc in range(2):
                hd = cg * 2 + cc
                b = next_bank()
                mm_group([(ps[b][:, :T], slabs[s][:, k, cc * 128:(cc + 1) * 128], h_t[:, k, :T], k == 0, k == KD - 1, False)
                          for k in range(KD)], reads=[slabB[s]] + hB, writes=[psB[b]])
                P.op("act", lambda e, b=b, hd=hd: e.activation(qt_t[:, hd, :T], ps[b][:, :T], AF.Copy, scale=ATT_SCALE),
                     reads=[psB[b]], writes=[qtB[hd]])
        for cg in range(4):
            s = load_slab("win", 0, 16, 1024 + cg * 256)
            for cc in range(2):
                hd = cg * 2 + cc
                b = next_bank()
                mm_group([(ps[b][:, :T], slabs[s][:, k, cc * 128:(cc + 1) * 128], h_t[:, k, :T], k == 0, k == KD - 1, False)
                          for k in range(KD)], reads=[slabB[s]] + hB, writes=[psB[b]])
                P.op("act", lambda e, b=b, hd=hd: e.activation(kt_t[:, hd, pos0:pos0 + T], ps[b][:, :T], AF.Copy),
                     reads=[psB[b]], writes=[ktB[hd]])
            tm_proj(s, T, kp[sl, pos0:pos0 + T, :], (sl, i, cg), kpB, cg * 256)
        for cg in range(4):
            s = load_slab("win", 0, 16, 2048 + cg * 256)
            tm_proj(s, T, vp[sl, pos0:pos0 + T, :], (sl, i, cg), vpB, cg * 256)
        def prefix_fn(c, xlc3, xlcB):
            if i == 0:
                P.op("dve", lambda e: e.memset(xlc3[:, :, 0:3], 0.0), writes=[xlcB])
            else:
                P.op("dve", lambda e: e.tensor_copy(xlc3[:, 0, 0:3], xprev[:, c, :]), reads=[xprevB[c]], writes=[xlcB])

        def last3_fn(c, xlc3, xlcB):
            P.op("dve", lambda e: e.tensor_copy(xprev[:, c, :], xlc3[:, 0, T:T + 3]), reads=[xlcB], writes=[xprevB[c]])

        def h0_fn(c, sg):
            if i == 0:
                return 0.0, constB
            return hprev[:, c:c + 1], hprevB[c]

        def hlast_fn(c, hs3, hsB):
            P.op("dve", lambda e: e.tensor_copy(hprev[:, c:c + 1], hs3[:, 0, T - 1:T]), reads=[hsB], writes=[hprevB[c]])

        def lru_phase():
            for c2 in range(4):
                sx = load_slab("win", 0, 16, 3072 + c2 * 256)
                sg_ = load_slab("win", 0, 16, 4096 + c2 * 256)
                yield
                for cc in range(2):
                    c = c2 * 2 + cc
                    bx_, bg_ = pool_bank("lru"), pool_bank("lru")
                    mm_group([(ps[bx_][:, :T], slabs[sx][:, k, cc * 128:(cc + 1) * 128], h_t[:, k, :T], k == 0, k == KD - 1, False)
                              for k in range(KD)], reads=[slabB[sx]] + hB, writes=[psB[bx_]])
                    mm_group([(ps[bg_][:, :T], slabs[sg_][:, k, cc * 128:(cc + 1) * 128], h_t[:, k, :T], k == 0, k == KD - 1, False)
                              for k in range(KD)], reads=[slabB[sg_]] + hB, writes=[psB[bg_]])
                    yield
                    yield from lru_chunk(c, T, 1, T, ps[bx_][:, :T], ps[bg_][:, :T], psB[bx_], psB[bg_],
                                         prefix_fn, h0_fn, hlast_fn, last3_fn)
            if i == n_tiles - 1:
                for c in range(8):
                    t = P.dma("pool", "sto", cpo[sl][:, c * 128:(c + 1) * 128].rearrange("j p -> p j"), xprev[:, c, :],
                              reads=[xprevB[c]])
                    out_toks.append(t)
                t = P.dma("pool", "sto", hpo[sl].rearrange("(c p) -> p c", p=128), hprev[:, :], reads=hprevB)
                out_toks.append(t)

        def att_phase():
            for hd in range(NH):
                yield from att_head(sl, i, hd)

        lt_split()
        interleave(lru_phase(), att_phase())
        lt_join()
        mix_out(T)
        rms_to_h(V_GF2, T)
        ffn("w2g", "w2u", "w2d", T)
        final_norm(T)
        store_y(yp[sl, pos0:pos0 + T, :], T)

    for sl in range(n_seq):
        for i in range(n_tiles):
            pass_state["ctr"] = 0
            prompt_tile(sl, i)
            pass_state["first"] = False

    def sample_tile():
        T = 16
        ktf = kt_t[:, :, :].rearrange("p h s -> p (h s)").bitcast(F32)
        slot = lambda k: ktf[:, k * 1024:(k + 1) * 1024]
        qtm, ktm, vtm, otm = slot(0), slot(1), slot(2), slot(7)
        kblk = [slot(3), slot(4)]
        vblk = [slot(5), slot(6)]
        qtmB, ktmB, vtmB, otmB = ktB[0], ktB[1], ktB[2], ktB[7]
        kblkB = [ktB[3], ktB[4]]
        vblkB = [ktB[5], ktB[6]]
        kblkNB = [P.buf("kblkN0"), P.buf("kblkN1")]
        vblkNB = [P.buf("vblkN0"), P.buf("vblkN1")]
        selm = qt_t[:, :, :].rearrange("p h s -> p (h s)").bitcast(F32)
        P.dma("pool", "smc", selm[:16, :], selm_d[:, :], writes=qtB)
        P.dma("pool", "smc", sconv_sb[:], sconv.rearrange("p c s j -> p (c s j)"), writes=scB)
        P.dma("pool", "smc", sh_sb[:], sh0.rearrange("p c s -> p (c s)"), writes=shB)

        load_x(xs[:, :], T)
        rms_to_h(V_GF1, T)
        ffn("w1g", "w1u", "w1d", T)
        rms_to_h(V_GMIX, T)

        def tm_s(s_slab, dst, dstB_, col0, scale):
            b = 5 + (col0 // 256) % 2
            mm_group([(ps[b][:T, 0:256], h_t[:, k, :T], slabs[s_slab][:, k, :], k == 0, k == KD - 1, False)
                      for k in range(KD)], reads=[slabB[s_slab]] + hB, writes=[psB[b]])
            P.op("act", lambda e: e.activation(dst[:T, col0:col0 + 256], ps[b][:T, 0:256], AF.Copy, scale=scale),
                 reads=[psB[b]], writes=[dstB_])

        for cg in range(4):
            s = load_slab("win", 0, 16, cg * 256)
            tm_s(s, qtm, qtmB, cg * 256, ATT_SCALE)
        for cg in range(4):
            s = load_slab("win", 0, 16, 1024 + cg * 256)
            tm_s(s, ktm, ktmB, cg * 256, 1.0)
        ksmB = P.buf("ksm")
        out_toks.append(P.dma("pool", "smo", ksm[:, :], ktm[:T, :], reads=[ktmB], writes=[ksmB]))
        for cg in range(4):
            s = load_slab("win", 0, 16, 2048 + cg * 256)
            tm_s(s, vtm, vtmB, cg * 256, 1.0)
        vsmB = P.buf("vsm")
        out_toks.append(P.dma("pool", "smo", vsm[:, :], vtm[:T, :], reads=[vtmB], writes=[vsmB]))

        sc4 = sconv_sb[:].rearrange("p (c s j) -> p c s j", c=8, s=4)
        sh3 = sh_sb[:].rearrange("p (c s) -> p c s", c=8)

        def prefix_fn(c, xlc3, xlcB):
            P.op("dve", lambda e: e.tensor_copy(xlc3[:, :, 0:3], sc4[:, c, :, :]), reads=[scB[c]], writes=[xlcB])

        def last3_fn(c, xlc3, xlcB):
            P.op("dve", lambda e: e.tensor_copy(sc4[:, c, :, :], xlc3[:, :, 4:7]), reads=[xlcB], writes=[scB[c]])

        def h0_fn(c, sg):
            return sh_sb[:, c * 4 + sg:c * 4 + sg + 1], shB[c]

        def hlast_fn(c, hs3, hsB):
            P.op("dve", lambda e: e.tensor_copy(sh3[:, c, :], hs3[:, :, 3]), reads=[hsB], writes=[shB[c]])

        lt_split()
        for c2 in range(4):
            sx = load_slab("win", 0, 16, 3072 + c2 * 256)
            sg_ = load_slab("win", 0, 16, 4096 + c2 * 256)
            for cc in range(2):
                c = c2 * 2 + cc
                bx_, bg_ = next_bank(), next_bank()
                mm_group([(ps[bx_][:, :T], slabs[sx][:, k, cc * 128:(cc + 1) * 128], h_t[:, k, :T], k == 0, k == KD - 1, False)
                          for k in range(KD)], reads=[slabB[sx]] + hB, writes=[psB[bx_]])
                mm_group([(ps[bg_][:, :T], slabs[sg_][:, k, cc * 128:(cc + 1) * 128], h_t[:, k, :T], k == 0, k == KD - 1, False)
                          for k in range(KD)], reads=[slabB[sg_]] + hB, writes=[psB[bg_]])
                for _ in lru_chunk(c, T, 4, 4, ps[bx_][:, :T], ps[bg_][:, :T], psB[bx_], psB[bg_],
                                   prefix_fn, h0_fn, hlast_fn, last3_fn):
                    pass
        lt_join()
        for c in range(8):
            out_toks.append(P.dma("pool", "sto", cso[:, :, c * 128:(c + 1) * 128].rearrange("s j p -> p s j"),
                                  sc4[:, c, :, :], reads=[scB[c]]))
            out_toks.append(P.dma("pool", "sto", hso[:, c * 128:(c + 1) * 128].rearrange("s p -> p s"),
                                  sh3[:, c, :], reads=[shB[c]]))

        selc, selcB = ar("selc", 0, 256)
        sbias, sbiasB = ar("sbias", 256, 24)
        b0t, b0B = ar("b0t", 280, 8)
        scs2 = [ar(f"scs{q}", 288 + 8 * q, 8) for q in range(2)]
        pm2 = [ar(f"pm{q}", 304 + 8 * q, 8) for q in range(2)]
        sc16, sc16B = ar("sc16", 320, 8)
        p16, p16B = ar("p16", 328, 8)
        rden, rdenB = ar("rden", 336, 8)
        prod2 = [ar(f"prod{q}", 512 + 1024 * q, 1024) for q in range(2)]
        wv2 = [ar(f"wv{q}", 2560 + 1024 * q, 1024) for q in range(2)]
        wvb2 = [(wv2[q][0].bitcast(BF16)[:, 0:1024], wv2[q][1]) for q in range(2)]
        selcb, selcbB = ar("selcb", 4736, 256, BF16)
        oh_all, oh_allB = ar("oh_all", 4608, 128)
        prod, prodB = prod2[0]
        wv, wvB = wv2[0]
        P.dma("pool", "smc", selc, selc_d[:, :], writes=[selcB])
        P.dma("pool", "smc", sbias, sbias_d[:, :], writes=[sbiasB])
        P.op("act", lambda e: e.activation(selcb, selc, AF.Copy), reads=[selcB], writes=[selcbB])
        P.dma("pool", "smc", b0t[:16, :], b0_d[:, :], writes=[b0B])
        h3 = lambda ap: ap.rearrange("p (h e) -> p h e", h=8)
        first = [True]
        ncomb = 0
        for s_ in range(4):
            for t in range(4):
                tok = s_ * 4 + t
                qb = (0, 1) if tok % 2 == 0 else (5, 6)
                mm_group([(ps[qb[0]][:, :], selm[:16, tok * 128:(tok + 1) * 128], qtm[:16, 0:512], True, True, False),
                          (ps[qb[1]][:, :], selm[:16, tok * 128:(tok + 1) * 128], qtm[:16, 512:1024], True, True, False)],
                         reads=[qtmB] + qtB, writes=[psB[qb[0]], psB[qb[1]]])
                for p in range(3):
                    sl_ = ncomb % 2
                    ncomb += 1
                    prod_, prodB_ = prod2[sl_]
                    wv_, wvB_ = wvb2[sl_]
                    scs, scsB = scs2[sl_]
                    pm, pmB = pm2[sl_]
                    for (blk, blkB, blkNB, cache, newd, newB, ch, qd) in (
                            (kblk[sl_], kblkB[sl_], kblkNB[sl_], ck, ksm, ksmB, f"kb{sl_}", "pool"),
                            (vblk[sl_], vblkB[sl_], vblkNB[sl_], cv, vsm, vsmB, f"vb{sl_}", "act")):
                        if p == 0:
                            P.dma(qd, ch, blk[0:128 - t, :], cache[s_, 1920 + t:2048, :], writes=[blkB])
                            if t > 0:
                                P.dma(qd, ch, blk[128 - t:128, :], newd[s_ * 4:s_ * 4 + t, :], reads=[newB], writes=[blkNB])
                        elif p == 1:
                            P.dma(qd, ch, blk[:, :], cache[s_, 1536 + t:2048:4, :], writes=[blkB, blkNB])
                        else:
                            P.dma(qd, ch, blk[:, :], cache[s_, t:2048:16, :], writes=[blkB, blkNB])
                    for hf in range(2):
                        P.op("dve", lambda e, hf=hf, sl_=sl_, prod_=prod_, qb=qb: e.tensor_tensor(
                            prod_[:, hf * 512:(hf + 1) * 512], kblk[sl_][:, hf * 512:(hf + 1) * 512], ps[qb[hf]][:, :], ALU.mult),
                            reads=[kblkB[sl_], kblkNB[sl_], psB[qb[hf]]], writes=[prodB_])
                    P.op("dve", lambda e, scs=scs, prod_=prod_: e.tensor_reduce(scs, h3(prod_), AX.X, ALU.add),
                         reads=[prodB_], writes=[scsB])
                    P.op("dve", lambda e, p=p, scs=scs: e.tensor_tensor(scs, scs, sbias[:, p * 8:(p + 1) * 8], ALU.add),
                         reads=[scsB, sbiasB], writes=[scsB])
                    P.op("act", lambda e, pm=pm, scs=scs: e.activation(pm, scs, AF.Exp), reads=[scsB], writes=[pmB])
                    P.op("dve", lambda e, sl_=sl_, wv_=wv_, pm=pm: e.tensor_tensor(
                        h3(wv_), h3(vblk[sl_]), pm.unsqueeze(2).broadcast_to([128, 8, 128]), ALU.mult),
                        reads=[vblkB[sl_], vblkNB[sl_], pmB], writes=[wvB_])
                    lhs = selc[:, tok * 16:(tok + 1) * 16]
                    lhsb = selcb[:, tok * 16:(tok + 1) * 16]
                    st = first[0]
                    first[0] = False
                    mm_group([(ps[2][:16, :], lhsb, wv_[:, 0:512], st, False, not st),
                              (ps[3][:16, :], lhsb, wv_[:, 512:1024], st, False, not st),
                              (ps[4][:16, 0:8], lhs, pm, st, False, not st)],
                             reads=[wvB_, pmB, selcB, selcbB], writes=[psB[2], psB[3], psB[4]])
        P.op("dve", lambda e: e.tensor_tensor(prod[:16, :], qtm[:16, :], ktm[:16, :], ALU.mult),
             reads=[qtmB, ktmB], writes=[prodB])
        P.op("dve", lambda e: e.tensor_reduce(sc16[:16, :], h3(prod[:16, :]), AX.X, ALU.add), reads=[prodB], writes=[sc16B])
        P.op("dve", lambda e: e.tensor_tensor(sc16[:16, :], sc16[:16, :], b0t[:16, :], ALU.add), reads=[sc16B, b0B], writes=[sc16B])
        P.op("act", lambda e: e.activation(p16[:16, :], sc16[:16, :], AF.Exp), reads=[sc16B], writes=[p16B])
        P.op("dve", lambda e: e.tensor_scalar(p16[:16, :], p16[:16, :], 3.0, None, ALU.mult), reads=[p16B], writes=[p16B])
        P.op("dve", lambda e: e.tensor_tensor(h3(wv[:16, :]), h3(vtm[:16, :]),
                                              p16[:16, :].unsqueeze(2).broadcast_to([16, 8, 128]), ALU.mult),
             reads=[vtmB, p16B], writes=[wvB])
        P.op("dve", lambda e: e.tensor_tensor(rden[:16, :], ps[4][:16, 0:8], p16[:16, :], ALU.add),
             reads=[psB[4], p16B], writes=[rdenB])
        P.op("dve", lambda e: e.reciprocal(rden[:16, :], rden[:16, :]), reads=[rdenB], writes=[rdenB])
        for hf in range(2):
            P.op("dve", lambda e, hf=hf: e.tensor_tensor(otm[:16, hf * 512:(hf + 1) * 512], ps[2 + hf][:16, :],
                                                         wv[:16, hf * 512:(hf + 1) * 512], ALU.add),
                 reads=[psB[2 + hf], wvB], writes=[otmB])
        P.op("dve", lambda e: e.tensor_tensor(h3(otm[:16, :]), h3(otm[:16, :]),
                                              rden[:16, :].unsqueeze(2).broadcast_to([16, 8, 128]), ALU.mult),
             reads=[otmB, rdenB], writes=[otmB])

        def fn_tr(e):
            ins = None
            for hd in range(NH):
                ins = e.transpose(ps[0][:, hd * 16:(hd + 1) * 16], otm[:16, hd * 128:(hd + 1) * 128], ident[:16, :16])
            return ins
        P.op("pe", fn_tr, reads=[otmB, constB], writes=[psB[0]])
        P.op("dve", lambda e: e.tensor_copy(oh_all, ps[0][:, 0:128]), reads=[psB[0]], writes=[oh_allB])
        for hd in range(NH):
            for _ in att_finish(hd, oh_all[:, hd * 16:(hd + 1) * 16], oh_allB, T):
                pass
        mix_out(T)
        rms_to_h(V_GF2, T)
        ffn("w2g", "w2u", "w2d", T)
        final_norm(T)
        store_y(ys[:, :], T)

    if do_sample:
        pass_state["ctr"] = 0
        sample_tile()

    P.wait_all("sp", out_toks)
    with nc.allow_non_contiguous_dma(reason="small state outputs / strided weight slabs"):
        P.emit(nc, es)
    es.close()
    return nc


def host_tables(rel_bias):
    rb = np.asarray(rel_bias, np.float32)
    btoep = np.full((128, 24, 256), NEG, np.float32)
    kk = np.arange(128)[:, None]
    qq = np.arange(128)[None, :]
    sbias = np.zeros((128, 3, 8), np.float32)
    for p, (w, dil) in enumerate(PATTERNS):
        bidx = rel_bucket_np(dil * np.arange(129, dtype=np.int32))
        j_own = qq - kk
        j_prev = 128 + qq - kk
        for h in range(8):
            f = rb[bidx, h]
            own = np.where(j_own >= 0, f[np.clip(j_own, 0, 128)], NEG)
            prev = np.where(j_prev <= 128, f[np.clip(j_prev, 0, 128)], NEG)
            btoep[:, p * 8 + h, 0:128] = own
            btoep[:, p * 8 + h, 128:256] = prev
            sbias[:, p, h] = f[128 - np.arange(128)]
    b0 = np.tile(rb[0][None, :], (16, 1)).astype(np.float32)
    return btoep.reshape(128, 24 * 256), sbias.reshape(128, 24), b0


def col_layout(v, k):
    return np.ascontiguousarray(np.asarray(v, np.float32).reshape(k, 128).T)


def make_in_maps(inp, ncores=NCORES):
    f32 = lambda a: np.ascontiguousarray(np.asarray(a, np.float32))
    vecs = np.zeros((128, NV), np.float32)
    vecs[:, V_GF1:V_GF1 + 16] = col_layout(inp["g_ffn1"][0], 16)
    vecs[:, V_GMIX:V_GMIX + 16] = col_layout(inp["g_mix"][0], 16)
    vecs[:, V_GF2:V_GF2 + 16] = col_layout(inp["g_ffn2"][0], 16)
    vecs[:, V_GFIN:V_GFIN + 16] = col_layout(inp["g_final"], 16)
    vecs[:, V_GATT:V_GATT + 8] = col_layout(inp["g_att_out"][0], 8)
    vecs[:, V_GLRU:V_GLRU + 8] = col_layout(inp["g_lru_out"][0], 8)
    cw = np.asarray(inp["conv_w"][0], np.float32)
    for j in range(4):
        vecs[:, V_CW + j * 8:V_CW + (j + 1) * 8] = col_layout(cw[j], 8)
    vecs[:, V_CB:V_CB + 8] = col_layout(inp["conv_b"][0], 8)
    vecs[:, V_BA:V_BA + 8] = col_layout(inp["b_a"][0], 8)
    vecs[:, V_BX:V_BX + 8] = col_layout(inp["b_x"][0], 8)
    vecs[:, V_LAM:V_LAM + 8] = col_layout(inp["lam"][0], 8)
    vecs[:, V_ONE] = 1.0
    btoep, sbias, b0 = host_tables(inp["rel_bias"])
    ident = np.eye(128, dtype=np.float32)
    selm = np.zeros((16, 16, 128), np.float32)
    selc = np.zeros((128, 16, 16), np.float32)
    for t in range(16):
        selm[t, t, :] = 1.0
        selc[:, t, t] = 1.0
    shared = {
        "w1g": f32(inp["w1_gate"][0]), "w1u": f32(inp["w1_up"][0]), "w1d": f32(inp["w1_down"][0]),
        "win": f32(inp["w_in"][0]), "wout": f32(inp["w_out"][0]),
        "w2g": f32(inp["w2_gate"][0]), "w2u": f32(inp["w2_up"][0]), "w2d": f32(inp["w2_down"][0]),
        "vecs": vecs, "wa": f32(inp["w_a"][0]), "wx": f32(inp["w_x"][0]),
        "btoep": btoep, "sbias": sbias, "b0": b0, "ident": ident,
        "selm": selm.reshape(16, 16 * 128), "selc": selc.reshape(128, 16 * 16),
    }
    xpr = np.asarray(inp["x_prompt"], np.float32)
    xsm = np.asarray(inp["x_sample"], np.float32)
    ckk = np.asarray(inp["cache_k"], np.float32)[0]
    cvv = np.asarray(inp["cache_v"], np.float32)[0]
    sc = np.asarray(inp["state_conv"], np.float32)[0]
    shh = np.asarray(inp["state_h"], np.float32)[0]
    maps = []
    for c in range(ncores):
        m = dict(shared)
        m["xp"] = np.ascontiguousarray(xpr[2 * c:2 * c + 2])
        m["xs"] = np.ascontiguousarray(xsm[4 * c:4 * c + 4].reshape(16, D))
        m["ck"] = np.ascontiguousarray(ckk[4 * c:4 * c + 4].reshape(4, 2048, 1024))
        m["cv"] = np.ascontiguousarray(cvv[4 * c:4 * c + 4].reshape(4, 2048, 1024))
        m["sconv"] = np.ascontiguousarray(sc[4 * c:4 * c + 4].reshape(4, 3, 8, 128).transpose(3, 2, 0, 1))
        m["sh0"] = np.ascontiguousarray(shh[4 * c:4 * c + 4].reshape(4, 8, 128).transpose(2, 1, 0))
        maps.append(m)
    return maps


_NC_CACHE = {}


def kernel(**inputs):
    cfg = {}
    nc = build(cfg)
    maps = make_in_maps(inputs)
    res = run_bass_kernel_spmd(nc, maps, core_ids=list(range(NCORES)))
    R = res.results
    cat = lambda k: np.concatenate([np.asarray(r[k]) for r in R], axis=0)
    y_prompt = cat("yp")
    y_sample = cat("ys").reshape(32, 4, D)
    k_prompt = cat("kp").reshape(1, 16, SEQ, 8, 128)
    v_prompt = cat("vp").reshape(1, 16, SEQ, 8, 128)
    conv_prompt = cat("cpo").reshape(1, 16, 3, 1024)
    h_prompt = cat("hpo").reshape(1, 16, 1024)
    k_sample = cat("ksm").reshape(1, 32, 4, 8, 128)
    v_sample = cat("vsm").reshape(1, 32, 4, 8, 128)
    conv_sample = cat("cso").reshape(1, 32, 3, 1024)
    h_sample = cat("hso").reshape(1, 32, 1024)
    return tuple(np.ascontiguousarray(a, dtype=np.float32) for a in (
        y_prompt, y_sample, k_prompt, v_prompt, conv_prompt, h_prompt, k_sample, v_sample, conv_sample, h_sample))
```
